# Optimizing a Trainium2 kernel written in Bass

```python
import math
import jax
import jax.numpy as jnp
from jax import lax
import numpy as np

D_MODEL = 1024
BATCH = 4
SEQ = 4096
DEPTH = 2

EPS = 1e-6
ROPE_THETA = 500000.0
Q_BLOCK = 128
D_FF = 2816
NEG_BIG = -1e30
FORCE_SCORE = 1e9

A_HEADS = 6
A_Q_RANK = 192
A_KV_RANK = 128
A_NOPE = 64
A_ROPE = 32
A_V = 64

B_HEADS = 6
B_KV_GROUPS = 2
B_HPG = B_HEADS // B_KV_GROUPS
B_DH = 64
B_ROT = B_DH // 4
CMP_LEN = 32
CMP_STRIDE = 16
SLC_LEN = 64
SLC_TOPN = 16
WINDOW = 512

C_HEADS = 4
C_DH = 32
C_ROT = C_DH // 4

MIX_WIDTH = A_HEADS * A_V + B_HEADS * B_DH + C_HEADS * 2 * C_DH
A_COLS = A_Q_RANK + A_KV_RANK + A_ROPE
B_COLS = B_HEADS * B_DH + 6 * B_KV_GROUPS * B_DH + 3 * B_HEADS
C_COLS = 3 * C_HEADS * 2 * C_DH
IN_COLS = A_COLS + B_COLS + C_COLS

kernel_name = 'hybrid_mla_nsa_diffattn_macaron'


def rmsnorm(x, g):
    xf = x.astype(jnp.float32)
    y = xf * lax.rsqrt(jnp.mean(xf * xf, axis=-1, keepdims=True) + EPS)
    return (y * g.astype(jnp.float32)).astype(x.dtype)


def rope_tables(positions, rot_dim):
    inv_freq = 1.0 / (ROPE_THETA ** (jnp.arange(0, rot_dim, 2, dtype=jnp.float32) / rot_dim))
    ang = positions.astype(jnp.float32)[..., None] * inv_freq
    return (jnp.cos(ang), jnp.sin(ang))


def apply_rope(x, cs):
    cos, sin = cs
    B, T, r2 = cos.shape
    shape = (B,) + (1,) * (x.ndim - 3) + (T, r2)
    cos = cos.reshape(shape)
    sin = sin.reshape(shape)
    xf = x.astype(jnp.float32)
    x1, x2 = xf[..., :r2], xf[..., r2:]
    return jnp.concatenate([x1 * cos - x2 * sin, x1 * sin + x2 * cos], axis=-1).astype(x.dtype)


def partial_rope(x, cs):
    r = 2 * cs[0].shape[-1]
    return jnp.concatenate([apply_rope(x[..., :r], cs), x[..., r:]], axis=-1)


def swiglu(x, wg, wu, wd):
    return (jax.nn.silu(x @ wg) * (x @ wu)) @ wd


def causal_block_attention(q, k, v, scale):
    B, H, T, dk = q.shape
    nb = T // Q_BLOCK
    q_blocks = jnp.moveaxis(q.reshape(B, H, nb, Q_BLOCK, dk), 2, 0)
    kpos = jnp.arange(T)

    def one_block(args):
        qi, i = args
        s = jnp.einsum('bhqd,bhkd->bhqk', qi, k, preferred_element_type=jnp.float32) * scale
        qpos = i * Q_BLOCK + jnp.arange(Q_BLOCK)
        s = jnp.where(kpos[None, :] <= qpos[:, None], s, -jnp.inf)
        p = jax.nn.softmax(s, axis=-1)
        return jnp.einsum('bhqk,bhkd->bhqd', p.astype(v.dtype), v)

    o = lax.map(one_block, (q_blocks, jnp.arange(nb)))
    return jnp.moveaxis(o, 0, 2).reshape(B, H, T, v.shape[-1])


def mla_mixer(h, cs_a, q_norm, kv_norm, w_uq, w_ukv):
    B, T, _ = h.shape
    c_q, c_kv, k_pe = jnp.split(h, [A_Q_RANK, A_Q_RANK + A_KV_RANK], axis=-1)
    q = (rmsnorm(c_q, q_norm) @ w_uq).reshape(B, T, A_HEADS, A_NOPE + A_ROPE).transpose(0, 2, 1, 3)
    kv = (rmsnorm(c_kv, kv_norm) @ w_ukv).reshape(B, T, A_HEADS, A_NOPE + A_V).transpose(0, 2, 1, 3)
    q = jnp.concatenate([q[..., :A_NOPE], apply_rope(q[..., A_NOPE:], cs_a)], axis=-1)
    k_pe = apply_rope(k_pe[:, None], cs_a)
    k = jnp.concatenate([kv[..., :A_NOPE], jnp.broadcast_to(k_pe, (B, A_HEADS, T, A_ROPE))], axis=-1)
    v = kv[..., A_NOPE:]
    o = causal_block_attention(q, k, v, (A_NOPE + A_ROPE) ** -0.5)
    return o.transpose(0, 2, 1, 3).reshape(B, T, A_HEADS * A_V)


def nsa_mixer(h, cs_b, pe_k, pe_v, phi_k1, phi_k2, phi_v1, phi_v2):
    B, T, _ = h.shape
    G, Hg, DH = B_KV_GROUPS, B_HPG, B_DH
    qw, kvw = B_HEADS * DH, G * DH
    offs = [qw + j * kvw for j in range(7)]
    q, kc, vc, ks, vs, kw, vw, gate_logits = jnp.split(h, offs, axis=-1)

    def split_kv(t):
        return t.reshape(B, T, G, DH).transpose(0, 2, 1, 3)

    q = partial_rope(q.reshape(B, T, G, Hg, DH).transpose(0, 2, 3, 1, 4), cs_b)
    kc, ks, kw = [partial_rope(split_kv(t), cs_b) for t in (kc, ks, kw)]
    vc, vs, vw = [split_kv(t) for t in (vc, vs, vw)]
    scale = DH ** -0.5
    tpos = jnp.arange(T)
    nb = T // Q_BLOCK

    n_cmp = (T - CMP_LEN) // CMP_STRIDE + 1
    cmp_idx = jnp.arange(n_cmp)[:, None] * CMP_STRIDE + jnp.arange(CMP_LEN)[None, :]

    def compress(t, pe, w1, w2):
        blk = (t[:, :, cmp_idx] + pe).reshape(B, G, n_cmp, CMP_LEN * DH)
        return jax.nn.silu(blk @ w1) @ w2

    k_cmp = compress(kc, pe_k, phi_k1, phi_k2)
    v_cmp = compress(vc, pe_v, phi_v1, phi_v2)
    s_cmp = jnp.einsum('bghtd,bgcd->bghtc', q, k_cmp, preferred_element_type=jnp.float32) * scale
    cmp_mask = cmp_idx[:, -1][None, :] <= tpos[:, None]
    p_cmp = jax.nn.softmax(jnp.where(cmp_mask, s_cmp, NEG_BIG), axis=-1) * cmp_mask
    o_cmp = jnp.einsum('bghtc,bgcd->bghtd', p_cmp.astype(v_cmp.dtype), v_cmp)

    n_slc = T // SLC_LEN
    top_n = min(SLC_TOPN, n_slc)
    blk_start = jnp.arange(n_slc) * SLC_LEN
    overlap = ((cmp_idx[:, 0][:, None] < blk_start[None, :] + SLC_LEN)
               & (cmp_idx[:, -1][:, None] >= blk_start[None, :])).astype(jnp.float32)
    imp = jnp.einsum('bghtc,cm->bgtm', p_cmp, overlap)
    cur = (tpos // SLC_LEN)[:, None]
    m = jnp.arange(n_slc)[None, :]
    forced = (m == 0) | (m == cur) | (m == cur - 1)
    score = jnp.where(forced, FORCE_SCORE, jnp.where(m <= cur, imp, -1.0))
    sel = lax.top_k(score, top_n)[1]

    ks_blk = ks.reshape(B, G, n_slc, SLC_LEN, DH)
    vs_blk = vs.reshape(B, G, n_slc, SLC_LEN, DH)
    q_blocks = jnp.moveaxis(q.reshape(B, G, Hg, nb, Q_BLOCK, DH), 3, 0)
    sel_blocks = jnp.moveaxis(sel.reshape(B, G, nb, Q_BLOCK, top_n), 2, 0)
    gather_blocks = jax.vmap(jax.vmap(lambda blocks, ids: blocks[ids]))

    def slc_block(args):
        qi, si, i = args
        kg = gather_blocks(ks_blk, si).reshape(B, G, Q_BLOCK, top_n * SLC_LEN, DH)
        vg = gather_blocks(vs_blk, si).reshape(B, G, Q_BLOCK, top_n * SLC_LEN, DH)
        kpos = (si[..., None] * SLC_LEN + jnp.arange(SLC_LEN)).reshape(B, G, Q_BLOCK, top_n * SLC_LEN)
        qpos = i * Q_BLOCK + jnp.arange(Q_BLOCK)
        valid = kpos <= qpos[:, None]
        s = jnp.einsum('bghqd,bgqkd->bghqk', qi, kg, preferred_element_type=jnp.float32) * scale
        p = jax.nn.softmax(jnp.where(valid[:, :, None], s, -jnp.inf), axis=-1)
        return jnp.einsum('bghqk,bgqkd->bghqd', p.astype(vg.dtype), vg)

    o_slc = lax.map(slc_block, (q_blocks, sel_blocks, jnp.arange(nb)))
    o_slc = jnp.moveaxis(o_slc, 0, 3).reshape(B, G, Hg, T, DH)

    kw_pad = jnp.pad(kw, ((0, 0), (0, 0), (WINDOW, 0), (0, 0)))
    vw_pad = jnp.pad(vw, ((0, 0), (0, 0), (WINDOW, 0), (0, 0)))
    band = jnp.arange(nb)[:, None] * Q_BLOCK + jnp.arange(WINDOW + Q_BLOCK)[None, :]
    kband = kw_pad[:, :, band]
    vband = vw_pad[:, :, band]
    kpos_w = band - WINDOW
    qpos_w = tpos.reshape(nb, Q_BLOCK)
    dist = qpos_w[:, :, None] - kpos_w[:, None, :]
    wmask = (dist >= 0) & (dist < WINDOW) & (kpos_w[:, None, :] >= 0)
    qwb = q.reshape(B, G, Hg, nb, Q_BLOCK, DH)
    s_w = jnp.einsum('bghnqd,bgnkd->bghnqk', qwb, kband, preferred_element_type=jnp.float32) * scale
    p_w = jax.nn.softmax(jnp.where(wmask, s_w, -jnp.inf), axis=-1)
    o_win = jnp.einsum('bghnqk,bgnkd->bghnqd', p_w.astype(vband.dtype), vband).reshape(B, G, Hg, T, DH)

    g = jax.nn.sigmoid(gate_logits.astype(jnp.float32)).reshape(B, T, G, Hg, 3)
    g = g.transpose(0, 2, 3, 1, 4).astype(q.dtype)
    o = g[..., 0:1] * o_cmp + g[..., 1:2] * o_slc + g[..., 2:3] * o_win
    return o.transpose(0, 3, 1, 2, 4).reshape(B, T, B_HEADS * DH)


def diff_mixer(h, cs_c, lq1, lk1, lq2, lk2, sub_norm, layer_idx):
    B, T, _ = h.shape
    q, k, v = jnp.split(h, 3, axis=-1)
    q = partial_rope(q.reshape(B, T, C_HEADS, 2, C_DH).transpose(0, 2, 3, 1, 4), cs_c)
    k = partial_rope(k.reshape(B, T, C_HEADS, 2, C_DH).transpose(0, 2, 3, 1, 4), cs_c)
    v = v.reshape(B, T, C_HEADS, 2 * C_DH).transpose(0, 2, 1, 3)
    lam_init = 0.8 - 0.6 * math.exp(-0.3 * layer_idx)
    f32 = jnp.float32
    lam = (jnp.exp(jnp.sum(lq1.astype(f32) * lk1.astype(f32)))
           - jnp.exp(jnp.sum(lq2.astype(f32) * lk2.astype(f32))) + lam_init)
    scale = C_DH ** -0.5
    o1 = causal_block_attention(q[:, :, 0], k[:, :, 0], v, scale)
    o2 = causal_block_attention(q[:, :, 1], k[:, :, 1], v, scale)
    o = o1.astype(f32) - lam * o2.astype(f32)
    o = (rmsnorm(o, sub_norm) * (1.0 - lam_init)).astype(h.dtype)
    return o.transpose(0, 2, 1, 3).reshape(B, T, C_HEADS * 2 * C_DH)


def setup_inputs(seed: int = 0) -> dict:
    key = jax.random.key(seed)
    keys = iter(jax.random.split(key, 40))
    f32 = jnp.float32
    L = DEPTH

    def nrm(shape, fan_in):
        return jax.random.normal(next(keys), shape, f32) * (fan_in ** -0.5)

    def gain(shape):
        return 1.0 + 0.02 * jax.random.normal(next(keys), shape, f32)

    def small(shape, s):
        return s * jax.random.normal(next(keys), shape, f32)

    x = jax.random.normal(next(keys), (BATCH, SEQ, D_MODEL), f32)
    offset = jax.random.randint(next(keys), (BATCH, 1), 0, 1024, dtype=jnp.int32)
    positions = offset + jnp.arange(SEQ, dtype=jnp.int32)[None, :]
    return {
        'x': x,
        'positions': positions,
        'ffn1_norm': gain((L, D_MODEL)),
        'ffn1_wg': nrm((L, D_MODEL, D_FF), D_MODEL),
        'ffn1_wu': nrm((L, D_MODEL, D_FF), D_MODEL),
        'ffn1_wd': nrm((L, D_FF, D_MODEL), D_FF),
        'mix_norm': gain((L, D_MODEL)),
        'w_in': nrm((L, D_MODEL, IN_COLS), D_MODEL),
        'mla_q_norm': gain((L, A_Q_RANK)),
        'mla_kv_norm': gain((L, A_KV_RANK)),
        'mla_w_uq': nrm((L, A_Q_RANK, A_HEADS * (A_NOPE + A_ROPE)), A_Q_RANK),
        'mla_w_ukv': nrm((L, A_KV_RANK, A_HEADS * (A_NOPE + A_V)), A_KV_RANK),
        'nsa_pe_k': small((L, CMP_LEN, B_DH), 0.1),
        'nsa_pe_v': small((L, CMP_LEN, B_DH), 0.1),
        'nsa_phi_k1': nrm((L, CMP_LEN * B_DH, B_DH), CMP_LEN * B_DH),
        'nsa_phi_k2': nrm((L, B_DH, B_DH), B_DH),
        'nsa_phi_v1': nrm((L, CMP_LEN * B_DH, B_DH), CMP_LEN * B_DH),
        'nsa_phi_v2': nrm((L, B_DH, B_DH), B_DH),
        'diff_lq1': small((L, C_DH), 0.1),
        'diff_lk1': small((L, C_DH), 0.1),
        'diff_lq2': small((L, C_DH), 0.1),
        'diff_lk2': small((L, C_DH), 0.1),
        'diff_sub_norm': gain((L, 2 * C_DH)),
        'w_out': nrm((L, MIX_WIDTH, D_MODEL), MIX_WIDTH),
        'ffn2_norm': gain((L, D_MODEL)),
        'ffn2_wg': nrm((L, D_MODEL, D_FF), D_MODEL),
        'ffn2_wu': nrm((L, D_MODEL, D_FF), D_MODEL),
        'ffn2_wd': nrm((L, D_FF, D_MODEL), D_FF),
        'final_norm': gain((D_MODEL,)),
    }


def reference(x, positions, ffn1_norm, ffn1_wg, ffn1_wu, ffn1_wd, mix_norm, w_in,
              mla_q_norm, mla_kv_norm, mla_w_uq, mla_w_ukv,
              nsa_pe_k, nsa_pe_v, nsa_phi_k1, nsa_phi_k2, nsa_phi_v1, nsa_phi_v2,
              diff_lq1, diff_lk1, diff_lq2, diff_lk2, diff_sub_norm, w_out,
              ffn2_norm, ffn2_wg, ffn2_wu, ffn2_wd, final_norm):
    cs_a = rope_tables(positions, A_ROPE)
    cs_b = rope_tables(positions, B_ROT)
    cs_c = rope_tables(positions, C_ROT)
    for l in range(DEPTH):
        x = x + 0.5 * swiglu(rmsnorm(x, ffn1_norm[l]), ffn1_wg[l], ffn1_wu[l], ffn1_wd[l])
        h = rmsnorm(x, mix_norm[l]) @ w_in[l]
        h_a, h_b, h_c = jnp.split(h, [A_COLS, A_COLS + B_COLS], axis=-1)
        o_a = mla_mixer(h_a, cs_a, mla_q_norm[l], mla_kv_norm[l], mla_w_uq[l], mla_w_ukv[l])
        o_b = nsa_mixer(h_b, cs_b, nsa_pe_k[l], nsa_pe_v[l], nsa_phi_k1[l], nsa_phi_k2[l],
                        nsa_phi_v1[l], nsa_phi_v2[l])
        o_c = diff_mixer(h_c, cs_c, diff_lq1[l], diff_lk1[l], diff_lq2[l], diff_lk2[l],
                         diff_sub_norm[l], l)
        o = jnp.concatenate([o_a, o_b, o_c], axis=-1)
        x = x + o @ w_out[l]
        x = x + 0.5 * swiglu(rmsnorm(x, ffn2_norm[l]), ffn2_wg[l], ffn2_wu[l], ffn2_wd[l])
    return rmsnorm(x, final_norm)
```

```python
import math
import os
from contextlib import ExitStack
import numpy as np
import concourse.bass as bass
import concourse.mybir as mybir
from concourse.bass_utils import run_bass_kernel_spmd

F32 = mybir.dt.float32
BF16 = mybir.dt.bfloat16
I32 = mybir.dt.int32
AF = mybir.ActivationFunctionType
ALU = mybir.AluOpType
AX = mybir.AxisListType

D_MODEL = 1024
D_FF = 2816
DEPTH = 2
EPS = 1e-6
THETA = 500000.0
NFC = D_FF // 128
WIN_NCH = 25
WIN_COLS = WIN_NCH * 128
WUQ_COLS = 7 * 128
NSMALL = 220

ENGS = ("sync", "act", "dve", "pool", "pe")
NDMA_SEM = 8


class StopBuild(Exception):
    pass


class Op:
    __slots__ = ("eng", "fn", "deps", "dma", "needs_inc", "sem", "val", "idx")

    def __init__(self, eng, fn, dma):
        self.eng = eng
        self.fn = fn
        self.deps = []
        self.dma = dma
        self.needs_inc = dma
        self.sem = None
        self.val = 0
        self.idx = 0


class Prog:
    def __init__(self):
        self.ops = {e: [] for e in ENGS}
        self.last_writer = {}
        self.readers = {}

    def op(self, eng, fn, reads=(), writes=(), dma=False):
        o = Op(eng, fn, dma)
        deps = set()
        for r in reads:
            lw = self.last_writer.get(r)
            if lw is not None:
                deps.add(lw)
        for w in writes:
            lw = self.last_writer.get(w)
            if lw is not None:
                deps.add(lw)
            for rd in self.readers.get(w, ()):
                deps.add(rd)
        for d in deps:
            if eng == "pe" and d.eng == "pe" and not d.dma:
                continue
            o.deps.append(d)
            d.needs_inc = True
        for w in writes:
            self.last_writer[w] = o
            self.readers[w] = []
        for r in reads:
            if r not in writes:
                self.readers.setdefault(r, []).append(o)
        self.ops[eng].append(o)
        return o

    def dma(self, q, out, in_, reads=(), writes=(), **kw):
        return self.op(q, lambda e: e.dma_start(out=out, in_=in_, **kw), reads, writes, dma=True)

    def barrier(self):
        lasts = []
        for e in ENGS:
            ops = self.ops[e]
            for o in reversed(ops):
                if not o.dma:
                    lasts.append(o)
                    break
            nd = 0
            for o in reversed(ops):
                if o.dma:
                    lasts.append(o)
                    nd += 1
                    if nd >= NDMA_SEM:
                        break
        for e in ENGS:
            o = Op(e, lambda eng: eng.nop(), False)
            for d in lasts:
                if e == "pe" and d.eng == "pe" and not d.dma:
                    continue
                o.deps.append(d)
                d.needs_inc = True
            self.ops[e].append(o)
        self.last_writer.clear()
        self.readers.clear()

    def emit(self, nc, stack):
        sems = {}
        dsems = {}
        for e in ENGS:
            sems[e] = stack.enter_context(nc.semaphore("s_" + e))
            if any(o.dma for o in self.ops[e]):
                dsems[e] = [stack.enter_context(nc.semaphore("d_%s%d" % (e, i))) for i in range(NDMA_SEM)]
        for e in ENGS:
            cnt = 0
            nd = 0
            for o in self.ops[e]:
                if o.dma:
                    o.sem = dsems[e][nd % NDMA_SEM]
                    o.val = 16 * (nd // NDMA_SEM + 1)
                    o.idx = nd
                    nd += 1
                elif o.needs_inc:
                    cnt += 1
                    o.sem = sems[e]
                    o.val = cnt
        all_ops = self.ops

        def run_engine(ename, eng):
            waited = {}
            dma_hist = []
            for o in all_ops[ename]:
                waits = {}
                for d in o.deps:
                    k = d.sem
                    if waits.get(id(k), (None, 0))[1] < d.val:
                        waits[id(k)] = (k, d.val)
                if o.dma:
                    if o.idx >= NDMA_SEM:
                        p = dma_hist[o.idx - NDMA_SEM]
                        if waits.get(id(p.sem), (None, 0))[1] < p.val:
                            waits[id(p.sem)] = (p.sem, p.val)
                    dma_hist.append(o)
                for kid, (k, v) in waits.items():
                    if waited.get(kid, 0) >= v:
                        continue
                    waited[kid] = v
                    eng.wait_ge(k, v)
                ins = o.fn(eng)
                if o.dma:
                    ins.then_inc(o.sem, 16)
                elif o.needs_inc:
                    ins.then_inc(o.sem, 1)
            for p in dma_hist[-NDMA_SEM:]:
                if waited.get(id(p.sem), 0) < p.val:
                    eng.wait_ge(p.sem, p.val)

        with nc.Block() as block:
            @block.sync
            def _(eng):
                run_engine("sync", eng)

            @block.scalar
            def _(eng):
                run_engine("act", eng)

            @block.vector
            def _(eng):
                run_engine("dve", eng)

            @block.gpsimd
            def _(eng):
                run_engine("pool", eng)

            @block.tensor
            def _(eng):
                run_engine("pe", eng)


class Arena:
    def __init__(self, handle, n):
        self.h = handle
        self.n = n
        self.off = 0
        self.mark = 0

    def set_mark(self):
        self.mark = self.off

    def reset(self):
        self.off = self.mark

    def alloc(self, free_shape, dt=BF16):
        nel = 1
        for s in free_shape:
            nel *= s
        nb = nel * (4 if dt in (F32, I32) else 2)
        nb = (nb + 63) // 64 * 64
        n16 = nb // 2
        assert self.off + n16 <= self.n, "SBUF arena overflow %d + %d > %d" % (self.off, n16, self.n)
        ap = self.h[:, self.off:self.off + n16]
        self.off += n16
        if dt != BF16:
            ap = ap.bitcast(dt)
        ap = ap[:, 0:nel]
        if len(free_shape) == 2:
            ap = ap.rearrange("p (a b) -> p a b", a=free_shape[0])
        elif len(free_shape) == 3:
            ap = ap.rearrange("p (a b c) -> p a b c", a=free_shape[0], b=free_shape[1])
        return ap


A_OFF, B_OFF, C_OFF = 0, 352, 1522


def _win_perm():
    ch = []

    def pad(lst):
        assert len(lst) <= 128
        return lst + [-1] * (128 - len(lst))

    ch.append(pad(list(range(0, 128))))
    ch.append(pad(list(range(128, 192))))
    ch.append(pad(list(range(192, 320))))
    ra, rap = [], []
    for h in range(6):
        for d in range(16):
            ra.append(B_OFF + 64 * h + d)
            rap.append(B_OFF + 64 * h + (d + 8) % 16)
    for d in range(32):
        ra.append(A_OFF + 320 + d)
        rap.append(A_OFF + 320 + (d + 16) % 32)
    ch.append(pad(ra))
    ch.append(pad(rap))
    rb, rbp = [], []
    for b in range(3):
        for g in range(2):
            base = B_OFF + 384 + 256 * b + 64 * g
            for d in range(16):
                rb.append(base + d)
                rbp.append(base + (d + 8) % 16)
    ch.append(pad(rb))
    ch.append(pad(rbp))
    rc, rcp = [], []
    for qk in range(2):
        for idx in range(8):
            base = C_OFF + 256 * qk + 32 * idx
            for d in range(8):
                rc.append(base + d)
                rcp.append(base + (d + 4) % 8)
    ch.append(pad(rc))
    ch.append(pad(rcp))
    for pr in range(3):
        l = []
        for h in (2 * pr, 2 * pr + 1):
            l += [B_OFF + 64 * h + d for d in range(16, 64)]
        ch.append(pad(l))
    for b in range(3):
        l = []
        for g in range(2):
            base = B_OFF + 384 + 256 * b + 64 * g
            l += [base + d for d in range(16, 64)]
        ch.append(pad(l))
    ch.append(pad([B_OFF + 512 + d for d in range(128)]))
    for qk in range(2):
        for half in range(2):
            l = []
            for idx in range(4 * half, 4 * half + 4):
                base = C_OFF + 256 * qk + 32 * idx
                l += [base + d for d in range(8, 32)]
            ch.append(pad(l))
    ch.append(pad([B_OFF + 1152 + d for d in range(18)]))
    ch.append(pad([B_OFF + 768 + d for d in range(128)]))
    ch.append(pad([B_OFF + 1024 + d for d in range(128)]))
    ch.append(pad([C_OFF + 512 + d for d in range(128)]))
    ch.append(pad([C_OFF + 640 + d for d in range(128)]))
    assert len(ch) == WIN_NCH
    return [c for l in ch for c in l]


def _wuq_perm():
    ch = []
    for pr in range(3):
        l = []
        for h in (2 * pr, 2 * pr + 1):
            l += [96 * h + d for d in range(64)]
        ch.append(l)
    r0, r0p, r1, r1p = [], [], [], []
    for h in range(4):
        for d in range(32):
            r0.append(96 * h + 64 + d)
            r0p.append(96 * h + 64 + (d + 16) % 32)
    for h in range(4, 6):
        for d in range(32):
            r1.append(96 * h + 64 + d)
            r1p.append(96 * h + 64 + (d + 16) % 32)
    r1 += [-1] * 64
    r1p += [-1] * 64
    ch += [r0, r0p, r1, r1p]
    return [c for l in ch for c in l]


def _wukv_perm():
    l = []
    for h in range(6):
        l += [128 * h + d for d in range(64)]
    for h in range(6):
        l += [128 * h + 64 + d for d in range(64)]
    return l


def _take_cols(w, perm):
    perm = np.asarray(perm)
    out = np.zeros((w.shape[0], len(perm)), dtype=w.dtype)
    m = perm >= 0
    out[:, m] = w[:, perm[m]]
    return out


def _rope_consts():
    def invf(rot):
        return (1.0 / (THETA ** (np.arange(0, rot, 2, dtype=np.float32) / np.float32(rot)))).astype(np.float32)
    fa, fb, fc = invf(32), invf(16), invf(8)
    out = np.zeros((128, 10), np.float32)

    def fill(rc, rows):
        for p, (f, sgn) in enumerate(rows):
            out[p, 2 * rc] = f
            out[p, 2 * rc + 1] = sgn * f
    ra = []
    for h in range(6):
        for d in range(16):
            ra.append((fb[d % 8], -1.0 if d < 8 else 1.0))
    for d in range(32):
        ra.append((fa[d % 16], -1.0 if d < 16 else 1.0))
    fill(0, ra)
    rb = []
    for _ in range(6):
        for d in range(16):
            rb.append((fb[d % 8], -1.0 if d < 8 else 1.0))
    fill(1, rb)
    rcl = []
    for _ in range(16):
        for d in range(8):
            rcl.append((fc[d % 4], -1.0 if d < 4 else 1.0))
    fill(2, rcl)
    q0 = []
    for _ in range(4):
        for d in range(32):
            q0.append((fa[d % 16], -1.0 if d < 16 else 1.0))
    fill(3, q0)
    fill(4, q0[:64])
    return out


def build(T, L=DEPTH, dbg=False):
    NG = T // 512
    NT = T // 128
    NCMP = T // 16 - 1
    NCC = (NCMP + 127) // 128
    NSLC = T // 64
    nc = bass.Bass("TRN2", target_bir_lowering=False)
    P = Prog()
    st = ExitStack()

    def din(name, shape, dt=F32):
        return nc.dram_tensor(name, list(shape), dt, kind="ExternalInput").ap()

    def dscr(name, shape, dt=BF16):
        return nc.dram_tensor(name, list(shape), dt).ap()

    x_in = din("x", [T, D_MODEL])
    pos_in = din("pos", [1, T], I32)
    fnorm_in = din("fnorm", [128, D_MODEL])
    ropef_in = din("ropef", [128, 10])
    W = []
    for l in range(L):
        d = {}
        for nm in ("wg1", "wu1", "wg2", "wu2"):
            d[nm] = din("%s_%d" % (nm, l), [D_MODEL, D_FF])
        for nm in ("wd1", "wd2"):
            d[nm] = din("%s_%d" % (nm, l), [D_FF, D_MODEL])
        d["win"] = din("win_%d" % l, [D_MODEL, WIN_COLS])
        d["wuq"] = din("wuq_%d" % l, [192, WUQ_COLS])
        d["wukv"] = din("wukv_%d" % l, [128, 768])
        d["wout"] = din("wout_%d" % l, [D_MODEL, D_MODEL])
        d["phik1"] = din("phik1_%d" % l, [2048, 64])
        d["phiv1"] = din("phiv1_%d" % l, [2048, 64])
        d["phik2"] = din("phik2_%d" % l, [64, 64])
        d["phiv2"] = din("phiv2_%d" % l, [64, 64])
        d["small"] = din("small_%d" % l, [128, NSMALL])
        W.append(d)
    y_out = nc.dram_tensor("y", [T, D_MODEL], F32, kind="ExternalOutput").ap()

    WS = []
    for l in range(L):
        d = {}
        for nm in ("wg1", "wu1", "wg2", "wu2"):
            d[nm] = dscr("s_%s_%d" % (nm, l), [D_MODEL, D_FF])
        for nm in ("wd1", "wd2"):
            d[nm] = dscr("s_%s_%d" % (nm, l), [D_FF, D_MODEL])
        d["win"] = dscr("s_win_%d" % l, [D_MODEL, WIN_COLS])
        d["wout"] = dscr("s_wout_%d" % l, [D_MODEL, D_MODEL])
        WS.append(d)
    xres = dscr("xres", [T, D_MODEL], F32)
    NFM = 23
    fm = dscr("fm", [NFM, 128, T])
    tmv = dscr("tmv", [T, 896])
    ot = dscr("ot", [8, 128, T])
    ropeT = dscr("ropeT", [10, 128, T], F32)
    dbg_out = {}
    if dbg:
        dbg_out["d_xres"] = nc.dram_tensor("d_xres", [T, D_MODEL], F32, kind="ExternalOutput").ap()
        dbg_out["d_fm"] = nc.dram_tensor("d_fm", [NFM, 128, T], BF16, kind="ExternalOutput").ap()
        dbg_out["d_tmv"] = nc.dram_tensor("d_tmv", [T, 896], BF16, kind="ExternalOutput").ap()
        dbg_out["d_ot"] = nc.dram_tensor("d_ot", [8, 128, T], BF16, kind="ExternalOutput").ap()

    FM_RA, FM_RB, FM_RC = 0, 1, 2
    FM_NQ, FM_NK, FM_VC, FM_DQ, FM_DK, FM_GT = 3, 6, 9, 10, 12, 14
    FM_QN, FM_QR, FM_KN = 15, 18, 20

    ARENA_N = 103 * 1024
    arena_h = nc.alloc_sbuf_tensor("arena", [128, ARENA_N], BF16)
    A = Arena(arena_h, ARENA_N)
    pb = [nc.alloc_psum_tensor("pb%d" % i, [128, 512], F32)[:] for i in range(8)]

    def MM(out, lhsT, rhs, start, stop, rd, wr):
        P.op("pe", lambda e: e.matmul(out, lhsT=lhsT, rhs=rhs, start=start, stop=stop), rd, wr)

    def ACT(out, in_, func, rd, wr, **kw):
        P.op("act", lambda e: e.activation(out=out, in_=in_, func=func, **kw), rd, wr)

    def TT(eng, out, in0, in1, op, rd, wr):
        P.op(eng, lambda e: e.tensor_tensor(out=out, in0=in0, in1=in1, op=op), rd, wr)

    def TS(eng, out, in0, s1, s2, op0, op1, rd, wr):
        if op1 is None:
            P.op(eng, lambda e: e.tensor_scalar(out=out, in0=in0, scalar1=s1, scalar2=None, op0=op0), rd, wr)
        else:
            P.op(eng, lambda e: e.tensor_scalar(out=out, in0=in0, scalar1=s1, scalar2=s2, op0=op0, op1=op1), rd, wr)

    def STT(eng, out, in0, scalar, in1, op0, op1, rd, wr):
        P.op(eng, lambda e: e.scalar_tensor_tensor(out=out, in0=in0, scalar=scalar, in1=in1, op0=op0, op1=op1), rd, wr)

    def CP(eng, out, in_, rd, wr):
        P.op(eng, lambda e: e.tensor_copy(out=out, in_=in_), rd, wr)

    def MS(eng, out, val, wr):
        P.op(eng, lambda e: e.memset(out, val), (), wr)

    def ASEL(out, in_, pattern, op, fill, base, cm, rd, wr):
        P.op("pool", lambda e: e.affine_select(out=out, in_=in_, pattern=pattern, compare_op=op, fill=fill,
                                                base=base, channel_multiplier=cm), rd, wr)

    def RECIP(out, in_, rd, wr):
        P.op("dve", lambda e: e.reciprocal(out=out, in_=in_), rd, wr)

    identf = A.alloc([128], F32)
    ident = A.alloc([128])
    onesb = A.alloc([128])
    tri = A.alloc([128])
    triw = A.alloc([128])
    ntri = A.alloc([128])
    ntriw = A.alloc([128])
    OV = A.alloc([NCC, 65])
    fnorm = A.alloc([D_MODEL], F32)
    ropef = A.alloc([10], F32)
    epsc = A.alloc([1], F32)
    tinyc = A.alloc([1], F32)
    smalls = [A.alloc([NSMALL], F32) for _ in range(L)]
    neglam = [A.alloc([1], F32) for _ in range(L)]
    gsub = [A.alloc([1], F32) for _ in range(L)]
    wuq_sb = [A.alloc([2, WUQ_COLS]) for _ in range(L)]
    wukv_sb = [A.alloc([768]) for _ in range(L)]
    phik2_sb = [A.alloc([64]) for _ in range(L)]
    phiv2_sb = [A.alloc([64]) for _ in range(L)]
    pekT = [A.alloc([32]) for _ in range(L)]
    pevT = [A.alloc([32]) for _ in range(L)]
    A.set_mark()

    MS("pool", identf, 1.0, ["identf"])
    ASEL(identf, identf, [[-1, 128]], ALU.is_equal, 0.0, 0, 1, ["identf"], ["identf"])
    CP("dve", ident, identf, ["identf"], ["ident"])
    MS("pool", onesb, 1.0, ["onesb"])
    MS("pool", tri, 1.0, ["tri"])
    ASEL(tri, tri, [[1, 128]], ALU.is_ge, 0.0, 0, -1, ["tri"], ["tri"])
    MS("pool", triw, 1.0, ["triw"])
    ASEL(triw, triw, [[-1, 128]], ALU.is_gt, 0.0, 0, 1, ["triw"], ["triw"])
    MS("pool", ntri, -30000.0, ["ntri"])
    ASEL(ntri, ntri, [[-1, 128]], ALU.is_gt, 0.0, 0, 1, ["ntri"], ["ntri"])
    MS("pool", ntriw, -30000.0, ["ntriw"])
    ASEL(ntriw, ntriw, [[1, 128]], ALU.is_ge, 0.0, 0, -1, ["ntriw"], ["ntriw"])
    MS("pool", OV, 1.0, ["OV"])
    for cc in range(NCC):
        ASEL(OV[:, cc, 0:64], OV[:, cc, 0:64], [[-4, 64]], ALU.is_ge, 0.0, 1 + 128 * cc, 1, ["OV"], ["OV"])
        ASEL(OV[:, cc, 0:64], OV[:, cc, 0:64], [[4, 64]], ALU.is_ge, 0.0, 3 - 128 * cc, -1, ["OV"], ["OV"])
    MS("dve", epsc, EPS, ["epsc"])
    MS("dve", tinyc, 1e-30, ["tinyc"])
    P.dma("sync", fnorm, fnorm_in, (), ["fnorm"])
    P.dma("sync", ropef, ropef_in, (), ["ropef"])
    for l in range(L):
        P.dma("sync", smalls[l], W[l]["small"], (), ["small%d" % l])
        P.dma("pool", wuq_sb[l][:, 0, :], W[l]["wuq"][0:128, :], (), ["wuq%d" % l])
        P.dma("pool", wuq_sb[l][0:64, 1, :], W[l]["wuq"][128:192, :], (), ["wuq%d" % l])
        P.dma("pool", wukv_sb[l], W[l]["wukv"], (), ["wukv%d" % l])
        P.dma("pool", phik2_sb[l][0:64], W[l]["phik2"], (), ["phik2%d" % l])
        P.dma("pool", phiv2_sb[l][0:64], W[l]["phiv2"], (), ["phiv2%d" % l])
        sm = smalls[l]
        CP("dve", pekT[l][0:64], sm[0:64, 156:188], ["small%d" % l], ["pekT%d" % l])
        CP("dve", pevT[l][0:64], sm[0:64, 188:220], ["small%d" % l], ["pevT%d" % l])
        lam_init = 0.8 - 0.6 * math.exp(-0.3 * l)
        tmp = A.alloc([32], F32)
        s12 = A.alloc([2], F32)
        TT("dve", tmp, sm[:, 28:60], sm[:, 60:92], ALU.mult, ["small%d" % l], ["lamtmp"])
        P.op("dve", lambda e, o=s12[:, 0:1], i=tmp: e.reduce_sum(out=o, in_=i, axis=AX.X), ["lamtmp"], ["s12a"])
        TT("dve", tmp, sm[:, 92:124], sm[:, 124:156], ALU.mult, ["small%d" % l, "s12a"], ["lamtmp"])
        P.op("dve", lambda e, o=s12[:, 1:2], i=tmp: e.reduce_sum(out=o, in_=i, axis=AX.X), ["lamtmp"], ["s12b"])
        ACT(s12, s12, AF.Exp, ["s12a", "s12b"], ["s12a", "s12b"])
        TT("dve", neglam[l], s12[:, 1:2], s12[:, 0:1], ALU.subtract, ["s12a", "s12b"], ["neglam%d" % l])
        TS("dve", neglam[l], neglam[l], -lam_init, None, ALU.add, None, ["neglam%d" % l], ["neglam%d" % l])
        TS("dve", gsub[l], sm[:, 27:28], 1.0 - lam_init, None, ALU.mult, None, ["small%d" % l], ["gsub%d" % l])
    A.set_mark()

    posi = A.alloc([T], I32)
    posf = A.alloc([T], F32)
    P.dma("sync", posi, pos_in.partition_broadcast(128), (), ["posi"])
    CP("dve", posf, posi, ["posi"], ["posf"])
    ra_ = [A.alloc([512], F32) for _ in range(2)]
    rk_ = [A.alloc([512], F32) for _ in range(2)]
    ri_ = [A.alloc([512], I32) for _ in range(2)]
    n = 0
    for tab in range(10):
        for g in range(NG):
            s = n % 2
            eng = "dve" if n % 2 == 0 else "pool"
            eng = "dve"
            n += 1
            a, k, ki = ra_[s], rk_[s], ri_[s]
            an, kn, kin = "ra%d" % s, "rk%d" % s, "ri%d" % s
            cs = slice(g * 512, (g + 1) * 512)
            if tab % 2 == 0:
                TS(eng, a, posf[:, cs], ropef[:, tab:tab + 1], math.pi / 2, ALU.mult, ALU.add, ["posf", "ropef"], [an])
            else:
                TS(eng, a, posf[:, cs], ropef[:, tab:tab + 1], None, ALU.mult, None, ["posf", "ropef"], [an])
            TS(eng, k, a, 1.0 / (2 * math.pi), None, ALU.mult, None, [an], [kn])
            CP(eng, ki, k, [kn], [kin])
            CP(eng, k, ki, [kin], [kn])
            STT(eng, a, k, -2 * math.pi, a, ALU.mult, ALU.add, [kn, an], [an])
            TS(eng, a, a, math.pi, -math.pi, ALU.min, ALU.max, [an], [an])
            ACT(a, a, AF.Sin, [an], [an])
            P.dma("sync", ropeT[tab, :, cs], a, [an], [])

    conv_cnt = [0]

    def convert_weights(stg_, items):
        for (l_, nm) in items:
            src = W[l_][nm]
            dst = WS[l_][nm]
            R, C = src.shape
            for r in range(R // 128):
                s = conv_cnt[0] % 3
                conv_cnt[0] += 1
                P.dma("pool", stg_[s][:, 0:C], src[r * 128:(r + 1) * 128, :], (), ["stg%d" % s])
                P.dma("pool", dst[r * 128:(r + 1) * 128, :], stg_[s][:, 0:C], ["stg%d" % s], [])

    stg = [A.alloc([WIN_COLS]) for _ in range(3)]
    convert_weights(stg, [(0, "wg1"), (0, "wu1"), (0, "wd1"), (0, "win")])
    P.barrier()
    A.reset()

    def row_phase(kind, l):
        A.reset()
        xs = [A.alloc([4, D_MODEL], F32) for _ in range(1)]
        xn = A.alloc([4, D_MODEL])
        xnT = A.alloc([8, 512])
        actT = A.alloc([NFC, 512])
        wslot = [A.alloc([8, 512]) for _ in range(3)]
        wd_sb = A.alloc([NFC, D_MODEL])
        wout_sb = A.alloc([8, D_MODEL]) if kind > 0 else None
        oin = A.alloc([8, 512]) if kind > 0 else None
        stage = [A.alloc([512]) for _ in range(4)]
        ss4 = A.alloc([4], F32)
        sg = [A.alloc([512], F32) for _ in range(2)]
        ftmp = [A.alloc([512], F32) for _ in range(4)]
        sqb = [A.alloc([512]) for _ in range(2)]
        cqn = A.alloc([2, 512])
        ckvn = A.alloc([512])
        ropeC = [A.alloc([512], F32) for _ in range(2)]
        ropeS = [A.alloc([512], F32) for _ in range(2)]
        cnt = {"ws": 0, "st": 0, "ft": 0, "pb": 0, "rp": 0}

        def next_pb():
            i = cnt["pb"] % 6
            cnt["pb"] += 1
            return pb[i], "pb%d" % i

        def next_stage():
            i = cnt["st"] % 4
            cnt["st"] += 1
            return stage[i], "stage%d" % i

        def next_ft():
            i = cnt["ft"] % 4
            cnt["ft"] += 1
            return ftmp[i], "ftmp%d" % i

        def load_wslot(src_ap_list):
            i = cnt["ws"] % 3
            cnt["ws"] += 1
            nm = "wslot%d" % i
            for (src, c0, rname) in src_ap_list:
                ncols = src.shape[1]
                P.dma("sync", wslot[i][:, :, c0:c0 + ncols], src.rearrange("(kc p) f -> p kc f", p=128), [], [nm])
            return wslot[i], nm

        def rmsnorm_T(xt, xname, gcol, gname):
            for tt in range(4):
                xr, xnr, sr = "%s_%d" % (xname, tt), "xn_%d" % tt, "ss4_%d" % tt
                ACT(xn[:, tt, :], xt[:, tt, :], AF.Square, [xr], [xnr, sr], accum_out=ss4[:, tt:tt + 1])
                ACT(ss4[:, tt:tt + 1], ss4[:, tt:tt + 1], AF.Sqrt, [sr, "epsc"], [sr], scale=1.0 / D_MODEL, bias=epsc[:, 0:1])
                RECIP(ss4[:, tt:tt + 1], ss4[:, tt:tt + 1], [sr], [sr])
                ACT(xn[:, tt, :], xt[:, tt, :], AF.Copy, [xr, sr], [xnr], scale=ss4[:, tt:tt + 1])
            for tt in range(4):
                pt_ = pb[6 + tt % 2][:, :].bitcast(BF16).rearrange("p (a b) -> p a b", a=8)
                pn = "pb%d" % (6 + tt % 2)
                for kc in range(8):
                    P.op("pe", lambda e, o=pt_[:, kc, :], i=xn[:, tt, kc * 128:(kc + 1) * 128]: e.transpose(out=o, in_=i, identity=ident),
                         ["xn_%d" % tt, "ident"], [pn])
                TT("dve", xnT[:, :, tt * 128:(tt + 1) * 128], pt_, gcol.unsqueeze(2).to_broadcast([128, 8, 128]),
                   ALU.mult, [pn, gname], ["xnT"])

        def ffn(xt, xname, wsd, l_, which):
            sm = smalls[l_]
            gcol = sm[:, 0:8] if which == 1 else sm[:, 16:24]
            wg, wu, wd = wsd["wg%d" % which], wsd["wu%d" % which], wsd["wd%d" % which]
            rmsnorm_T(xt, xname, gcol, "small%d" % l_)
            rg, ru, rd_ = "ws_wg%d_%d" % (which, l_), "ws_wu%d_%d" % (which, l_), "ws_wd%d_%d" % (which, l_)
            for fc2 in range(NFC // 2):
                wt, wn = load_wslot([(wg[:, fc2 * 256:(fc2 + 1) * 256], 0, rg), (wu[:, fc2 * 256:(fc2 + 1) * 256], 256, ru)])
                if fc2 == 2:
                    P.dma("sync", wd_sb, wd.rearrange("(fc p) d -> p fc d", p=128), [], ["wd_sb"])
                for sub in range(2):
                    fc = 2 * fc2 + sub
                    gp, gpn = next_pb()
                    up, upn = next_pb()
                    for kc in range(8):
                        MM(gp, wt[:, kc, sub * 128:(sub + 1) * 128], xnT[:, kc, :], kc == 0, kc == 7, [wn, "xnT"], [gpn])
                    for kc in range(8):
                        MM(up, wt[:, kc, 256 + sub * 128:256 + (sub + 1) * 128], xnT[:, kc, :], kc == 0, kc == 7, [wn, "xnT"], [upn])
                    s_ = sg[fc % 2]
                    sn = "sg%d" % (fc % 2)
                    ACT(s_, gp, AF.Silu, [gpn], [sn])
                    TT("dve", actT[:, fc, :], up, s_, ALU.mult, [upn, sn], ["actT%d" % fc])
            for tt in range(4):
                for dh in range(2):
                    yp, ypn = next_pb()
                    for fc in range(NFC):
                        MM(yp, actT[:, fc, tt * 128:(tt + 1) * 128], wd_sb[:, fc, dh * 512:(dh + 1) * 512], fc == 0, fc == NFC - 1,
                           ["actT%d" % fc, "wd_sb"], [ypn])
                    xsl = xt[:, tt, dh * 512:(dh + 1) * 512]
                    STT("dve", xsl, yp, 0.5, xsl, ALU.mult, ALU.add, [ypn, "%s_%d" % (xname, tt)], ["%s_%d" % (xname, tt)])

        def load_rope(rc, g):
            i = cnt["rp"] % 2
            cnt["rp"] += 1
            cs = slice(g * 512, (g + 1) * 512)
            P.dma("sync", ropeC[i], ropeT[2 * rc, :, cs], [], ["ropeC%d" % i])
            P.dma("sync", ropeS[i], ropeT[2 * rc + 1, :, cs], [], ["ropeS%d" % i])
            return ropeC[i], "ropeC%d" % i, ropeS[i], "ropeS%d" % i

        def rope_evac(mp, mpn, pp, ppn, rc, g, rows, fmid):
            C_, cn, S_, sn = load_rope(rc, g)
            t1, t1n = next_ft()
            t2, t2n = next_ft()
            so, son = next_stage()
            TT("dve", t1[0:rows], mp[0:rows], C_[0:rows], ALU.mult, [mpn, cn], [t1n])
            TT("dve", t2[0:rows], pp[0:rows], S_[0:rows], ALU.mult, [ppn, sn], [t2n])
            TT("dve", so[0:rows], t1[0:rows], t2[0:rows], ALU.add, [t1n, t2n], [son])
            P.dma("sync", fm[fmid, 0:rows, g * 512:(g + 1) * 512], so[0:rows], [son], [])

        def plain_evac(mp, mpn, rows, fmid, g, func=AF.Copy):
            so, son = next_stage()
            ACT(so[0:rows], mp[0:rows], func, [mpn], [son])
            P.dma("sync", fm[fmid, 0:rows, g * 512:(g + 1) * 512], so[0:rows], [son], [])

        def feat_rms(pslist, rowslist, gcols, gname, N, outs, outname):
            ssp, sspn = next_pb()
            for i, ((p_, pn_), rows) in enumerate(zip(pslist, rowslist)):
                sq = sqb[i]
                ACT(sq[0:rows], p_[0:rows], AF.Square, [pn_], ["sqb%d" % i])
                MM(ssp, onesb[0:rows, :], sq[0:rows], i == 0, i == len(pslist) - 1, ["sqb%d" % i, "onesb"], [sspn])
            rs, rsn = next_ft()
            ACT(rs, ssp, AF.Sqrt, [sspn, "epsc"], [rsn], scale=1.0 / N, bias=epsc[:, 0:1])
            RECIP(rs, rs, [rsn], [rsn])
            for i, ((p_, pn_), rows) in enumerate(zip(pslist, rowslist)):
                STT("dve", outs[i][0:rows], p_[0:rows], gcols[i][0:rows], rs[0:rows], ALU.mult, ALU.mult,
                    [pn_, gname, rsn], [outname])

        def win_step(xt, xname, l_, g, after_norm=None):
            sm = smalls[l_]
            wsrc = WS[l_]["win"]
            wr = "ws_win_%d" % l_
            slots = {}

            def ensure_slot(si):
                if si not in slots and si * 512 < WIN_COLS:
                    c0 = si * 512
                    c1 = min(WIN_COLS, c0 + 512)
                    slots[si] = load_wslot([(wsrc[:, c0:c1], 0, wr)])

            ensure_slot(0)
            ensure_slot(1)
            rmsnorm_T(xt, xname, sm[:, 8:16], "small%d" % l_)
            if after_norm is not None:
                after_norm()

            def chunk_w(ci):
                si = ci // 4
                ensure_slot(si)
                ensure_slot(si + 1)
                wt, wn = slots[si]
                return wt[:, :, (ci % 4) * 128:(ci % 4 + 1) * 128], wn

            def proj(ci, rows=128):
                wv, wn = chunk_w(ci)
                p_, pn_ = next_pb()
                for kc in range(8):
                    MM(p_[0:rows], wv[:, kc, 0:rows], xnT[:, kc, :], kc == 0, kc == 7, [wn, "xnT"], [pn_])
                return p_, pn_

            p0 = proj(0)
            p1 = proj(1, 64)
            feat_rms([p0, p1], [128, 64], [sm[:, 24:25], sm[:, 25:26]], "small%d" % l_, 192.0,
                     [cqn[:, 0, :], cqn[:, 1, :]], "cqn")
            p2 = proj(2)
            feat_rms([p2], [128], [sm[:, 26:27]], "small%d" % l_, 128.0, [ckvn], "ckvn")
            wq = wuq_sb[l_]
            wqn = "wuq%d" % l_

            def qproj(ci, rows=128):
                p_, pn_ = next_pb()
                MM(p_[0:rows], wq[:, 0, ci * 128:ci * 128 + rows], cqn[:, 0, :], True, False, [wqn, "cqn"], [pn_])
                MM(p_[0:rows], wq[0:64, 1, ci * 128:ci * 128 + rows], cqn[0:64, 1, :], False, True, [wqn, "cqn"], [pn_])
                return p_, pn_
            for pr in range(3):
                p_, pn_ = qproj(pr)
                plain_evac(p_, pn_, 128, FM_QN + pr, g)
            m_, mn_ = qproj(3)
            pp_, ppn_ = qproj(4)
            rope_evac(m_, mn_, pp_, ppn_, 3, g, 128, FM_QR)
            m_, mn_ = qproj(5, 64)
            pp_, ppn_ = qproj(6, 64)
            rope_evac(m_, mn_, pp_, ppn_, 4, g, 64, FM_QR + 1)
            wkv = wukv_sb[l_]
            wkvn = "wukv%d" % l_
            for pr in range(3):
                p_, pn_ = next_pb()
                MM(p_, wkv[:, pr * 128:(pr + 1) * 128], ckvn, True, True, [wkvn, "ckvn"], [pn_])
                plain_evac(p_, pn_, 128, FM_KN + pr, g)
            for tt in range(4):
                p_, pn_ = next_pb()
                MM(p_[:, 0:384], ckvn[:, tt * 128:(tt + 1) * 128], wkv[:, 384:768], True, True, [wkvn, "ckvn"], [pn_])
                so, son = next_stage()
                ACT(so[:, 0:384], p_[:, 0:384], AF.Copy, [pn_], [son])
                r0 = g * 512 + tt * 128
                P.dma("sync", tmv[r0:r0 + 128, 512:896], so[:, 0:384], [son], [])
            for (cm, cp_, rc, rows, fmid) in ((3, 4, 0, 128, FM_RA), (5, 6, 1, 96, FM_RB), (7, 8, 2, 128, FM_RC)):
                m_, mn_ = proj(cm, rows)
                pp_, ppn_ = proj(cp_, rows)
                rope_evac(m_, mn_, pp_, ppn_, rc, g, rows, fmid)
            for (ci, rows, fmid) in ((9, 96, FM_NQ), (10, 96, FM_NQ + 1), (11, 96, FM_NQ + 2),
                                     (12, 96, FM_NK), (13, 96, FM_NK + 1), (14, 96, FM_NK + 2),
                                     (15, 128, FM_VC), (16, 96, FM_DQ), (17, 96, FM_DQ + 1),
                                     (18, 96, FM_DK), (19, 96, FM_DK + 1)):
                p_, pn_ = proj(ci, rows)
                plain_evac(p_, pn_, rows, fmid, g)
            p_, pn_ = proj(20, 18)
            plain_evac(p_, pn_, 18, FM_GT, g, AF.Sigmoid)
            chunk_w(21)
            wt, wn = slots[21 // 4]
            assert 21 % 4 == 1
            chunk_w(24)
            wt2, wn2 = slots[6]
            for tt in range(4):
                p_, pn_ = next_pb()
                for kc in range(8):
                    MM(p_[:, 0:384], xnT[:, kc, tt * 128:(tt + 1) * 128], wt[:, kc, 128:512], kc == 0, kc == 7, [wn, "xnT"], [pn_])
                for kc in range(8):
                    MM(p_[:, 384:512], xnT[:, kc, tt * 128:(tt + 1) * 128], wt2[:, kc, 0:128], kc == 0, kc == 7, [wn2, "xnT"], [pn_])
                so, son = next_stage()
                ACT(so, p_, AF.Copy, [pn_], [son])
                r0 = g * 512 + tt * 128
                P.dma("sync", tmv[r0:r0 + 128, 0:512], so, [son], [])

        def wout_step(xt, xname, l_, g):
            for tt in range(4):
                for dh in range(2):
                    yp, ypn = next_pb()
                    for kc in range(8):
                        MM(yp, oin[:, kc, tt * 128:(tt + 1) * 128], wout_sb[:, kc, dh * 512:(dh + 1) * 512], kc == 0, kc == 7,
                           ["oin", "wout_sb"], [ypn])
                    xsl = xt[:, tt, dh * 512:(dh + 1) * 512]
                    TT("dve", xsl, yp, xsl, ALU.add, [ypn, "%s_%d" % (xname, tt)], ["%s_%d" % (xname, tt)])

        if kind > 0:
            P.dma("sync", wout_sb, WS[l - 1]["wout"].rearrange("(kc p) d -> p kc d", p=128), [], ["wout_sb"])
        xt = xs[0]
        xname = "xs0"
        xregs = ["%s_%d" % (xname, t_) for t_ in range(4)]
        src = x_in if kind == 0 else xres
        dst = y_out if kind == 2 else xres

        def load_x(g_):
            P.dma("sync", xt, src[g_ * 512:(g_ + 1) * 512, :].rearrange("(tt p) d -> p tt d", p=128), [], xregs)
            if kind > 0:
                P.dma("sync", oin, ot[:, :, g_ * 512:(g_ + 1) * 512].rearrange("c p t -> p c t"), [], ["oin"])

        def store_x(g_):
            P.dma("sync", dst[g_ * 512:(g_ + 1) * 512, :].rearrange("(tt p) d -> p tt d", p=128), xt, xregs, [])

        def mk_prefetch(g_):
            def f():
                if kind < 2:
                    store_x(g_)
                if g_ + 1 < NG:
                    load_x(g_ + 1)
            return f

        load_x(0)
        for g in range(NG):
            if kind == 0:
                ffn(xt, xname, WS[0], 0, 1)
                win_step(xt, xname, 0, g, mk_prefetch(g))
            elif kind == 1:
                wout_step(xt, xname, l - 1, g)
                ffn(xt, xname, WS[l - 1], l - 1, 2)
                ffn(xt, xname, WS[l], l, 1)
                win_step(xt, xname, l, g, mk_prefetch(g))
            else:
                wout_step(xt, xname, l - 1, g)
                ffn(xt, xname, WS[l - 1], l - 1, 2)
                for tt in range(4):
                    xr, xnr, sr = "%s_%d" % (xname, tt), "xn_%d" % tt, "ss4_%d" % tt
                    ACT(xn[:, tt, :], xt[:, tt, :], AF.Square, [xr], [xnr, sr], accum_out=ss4[:, tt:tt + 1])
                    ACT(ss4[:, tt:tt + 1], ss4[:, tt:tt + 1], AF.Sqrt, [sr, "epsc"], [sr], scale=1.0 / D_MODEL, bias=epsc[:, 0:1])
                    RECIP(ss4[:, tt:tt + 1], ss4[:, tt:tt + 1], [sr], [sr])
                    STT("dve", xt[:, tt, :], xt[:, tt, :], ss4[:, tt:tt + 1], fnorm, ALU.mult, ALU.mult,
                        [xr, sr, "fnorm"], [xr])
                store_x(g)
                mk_prefetch(g)()
        P.barrier()

    def attn_phase(l):
        A.reset()
        stg_a = [A.alloc([WIN_COLS]) for _ in range(3)]
        items = [(l, "wout"), (l, "wg2"), (l, "wu2"), (l, "wd2")]
        if l + 1 < L:
            items += [(l + 1, "wg1"), (l + 1, "wu1"), (l + 1, "wd1"), (l + 1, "win")]
        convert_weights(stg_a, items)
        sm = smalls[l]
        smn = "small%d" % l
        LA = int(os.environ.get("KLA", "2"))
        pipe = []

        def pipe_push(fn):
            pipe.append(fn)
            while len(pipe) > LA:
                pipe.pop(0)()

        def pipe_flush():
            while pipe:
                pipe.pop(0)()

        banks = {}
        cnt = {}
        tiles = {}

        def set_banks(ns, no, nm):
            banks["s"] = [(pb[i], "pb%d" % i) for i in range(ns)]
            banks["o"] = [(pb[ns + i], "pb%d" % (ns + i)) for i in range(no)]
            banks["m"] = [(pb[ns + no + i], "pb%d" % (ns + no + i)) for i in range(nm)]
            assert ns + no + nm <= 8
            for k in ("s", "o", "m", "rz", "os"):
                cnt[k] = 0
            tiles["pt"] = [A.alloc([512]) for _ in range(ns)]
            tiles["rz"] = [A.alloc([512], F32) for _ in range(3)]
            tiles["os"] = [A.alloc([512]) for _ in range(3)]

        def nxt(key, n):
            i = cnt[key] % n
            cnt[key] += 1
            return i

        def next_s():
            i = nxt("s", len(banks["s"]))
            return banks["s"][i][0], banks["s"][i][1], tiles["pt"][i], "pt%d" % i

        def next_o():
            return banks["o"][nxt("o", len(banks["o"]))]

        def next_m():
            return banks["m"][nxt("m", len(banks["m"]))]

        def next_rz():
            i = nxt("rz", 3)
            return tiles["rz"][i], "rzt%d" % i

        def next_os():
            i = nxt("os", 3)
            return tiles["os"][i], "ostage%d" % i

        def mk_pv(O, On, c0, c1, Vp, vname, kc, Pt, Ptn, first, last, fin):
            def pv():
                MM(O[:, c0:c1], Vp[:, kc, :], Pt[:, c0:c1], first, last, [vname, Ptn], [On])
                if last and fin is not None:
                    fin()
            return pv

        def dense_causal(kT, kname, qT, qname, dk, kb, Vp, vname, scale, qg, O, On, extra=None, win=False, fin=None):
            chunks = []
            if win:
                for j in range(4):
                    chunks.append((4 * qg + j, "d", j))
                if qg > 0:
                    for j in range(4):
                        chunks.append((4 * qg - 4 + j, "w", j))
            else:
                for kc in range(4 * qg):
                    chunks.append((kc, "f", 0))
                for j in range(4):
                    chunks.append((4 * qg + j, "d", j))
            assert chunks[0][1] == "f" or (chunks[0][1] == "d" and chunks[0][2] == 0)
            nchunks = len(chunks)
            q0 = qg * 512
            for idx, (kc, typ, j) in enumerate(chunks):
                if typ == "f":
                    c0, c1 = 0, 512
                elif typ == "d":
                    c0, c1 = 128 * j, 512
                else:
                    c0, c1 = 0, 128 * (j + 1)
                S, Sn, Pt, Ptn = next_s()
                has_extra = extra is not None
                has_mask = typ in ("d", "w")
                MM(S[:, c0:c1], kT[kb:kb + dk, kc * 128:(kc + 1) * 128], qT[kb:kb + dk, q0 + c0:q0 + c1], True,
                   not (has_extra or has_mask), [kname, qname], [Sn])
                if has_extra:
                    extra(kc, S, Sn, c0, c1, not has_mask)
                if has_mask:
                    mt, mtn = (ntri, "ntri") if typ == "d" else (ntriw, "ntriw")
                    MM(S[:, 128 * j:128 * j + 128], ident, mt, False, True, ["ident", mtn], [Sn])
                ACT(Pt[:, c0:c1], S[:, c0:c1], AF.Exp, [Sn], [Ptn], scale=scale)
                pipe_push(mk_pv(O, On, c0, c1, Vp, vname, kc, Pt, Ptn, idx == 0, idx == nchunks - 1, fin))

        def make_vp(Vp, vpname, Vall, vallname, c0):
            CP("dve", Vp[:, :, 0:64], Vall[:, :, c0:c0 + 64], [vallname], [vpname])

        set_banks(3, 3, 2)
        Vall = A.alloc([NT, 384])
        P.dma("sync", Vall, tmv[:, 512:896].rearrange("(tt p) c -> p tt c", p=128), [], ["Vall"])
        qTs = [A.alloc([T]) for _ in range(2)]
        kTs = [A.alloc([T]) for _ in range(2)]
        Vps = [A.alloc([NT, 128]) for _ in range(2)]
        for i in range(2):
            MS("dve", Vps[i], 1.0, ["Vp%d" % i])
        sc_a = 96.0 ** -0.5

        def mla_load(h):
            s = h % 2
            qT, kT, Vp = qTs[s], kTs[s], Vps[s]
            qn, kn, vn = "qT%d" % s, "kT%d" % s, "Vp%d" % s
            P.dma("sync", qT[0:64], fm[FM_QN + h // 2, 64 * (h % 2):64 * (h % 2) + 64, :], [], [qn])
            if h < 4:
                P.dma("sync", qT[64:96], fm[FM_QR, 32 * h:32 * h + 32, :], [], [qn])
            else:
                P.dma("sync", qT[64:96], fm[FM_QR + 1, 32 * (h - 4):32 * (h - 4) + 32, :], [], [qn])
            P.dma("sync", kT[0:64], fm[FM_KN + h // 2, 64 * (h % 2):64 * (h % 2) + 64, :], [], [kn])
            P.dma("sync", kT[64:96], fm[FM_RA, 96:128, :], [], [kn])
            make_vp(Vp, vn, Vall, "Vall", 64 * h)

        def mla_fin(O, On, h, qg):
            def fin():
                rz, rzn = next_rz()
                RECIP(rz[64:128], O[64:128], [On], [rzn])
                os_, osn = next_os()
                TT("dve", os_[0:64], O[0:64], rz[64:128], ALU.mult, [On, rzn], [osn])
                P.dma("sync", ot[h // 2, 64 * (h % 2):64 * (h % 2) + 64, qg * 512:(qg + 1) * 512], os_[0:64], [osn], [])
            return fin

        mla_load(0)
        for h in range(6):
            if h + 1 < 6:
                pipe_flush()
                mla_load(h + 1)
            s = h % 2
            for qg in range(NG):
                O, On = next_o()
                dense_causal(kTs[s], "kT%d" % s, qTs[s], "qT%d" % s, 96, 0, Vps[s], "Vp%d" % s, sc_a, qg, O, On,
                             fin=mla_fin(O, On, h, qg))
        pipe_flush()
        P.barrier()
        if os.environ.get("KSTOP") == "mla":
            raise StopBuild()

        A.reset()
        set_banks(3, 4, 1)
        Vall = A.alloc([NT, 256])
        P.dma("sync", Vall, tmv[:, 256:512].rearrange("(tt p) c -> p tt c", p=128), [], ["Vall"])
        qTs = [A.alloc([T]) for _ in range(2)]
        kTs = [[A.alloc([T]) for _ in range(2)] for _ in range(2)]
        Vps = [A.alloc([NT, 128]) for _ in range(2)]
        o12 = [[A.alloc([512], F32) for _ in range(2)] for _ in range(2)]
        dds = [A.alloc([512], F32) for _ in range(2)]
        sqds = [A.alloc([512]) for _ in range(2)]
        for i in range(2):
            MS("dve", Vps[i], 1.0, ["Vp%d" % i])
            MS("dve", qTs[i], 0.0, ["qT%d" % i])
            for m_ in range(2):
                MS("dve", kTs[i][m_], 0.0, ["kT%d_%d" % (i, m_)])
        sc_c = 32.0 ** -0.5

        def diff_load(hh):
            s = hh % 2
            qT, Vp = qTs[s], Vps[s]
            qn, vn = "qT%d" % s, "Vp%d" % s
            for mm_ in range(2):
                idx = 2 * hh + mm_
                kT, kn = kTs[s][mm_], "kT%d_%d" % (s, mm_)
                P.dma("sync", qT[32 * mm_:32 * mm_ + 8], fm[FM_RC, 8 * idx:8 * idx + 8, :], [], [qn])
                P.dma("sync", qT[32 * mm_ + 8:32 * mm_ + 32], fm[FM_DQ + idx // 4, 24 * (idx % 4):24 * (idx % 4) + 24, :], [], [qn])
                P.dma("sync", kT[32 * mm_:32 * mm_ + 8], fm[FM_RC, 64 + 8 * idx:64 + 8 * idx + 8, :], [], [kn])
                P.dma("sync", kT[32 * mm_ + 8:32 * mm_ + 32], fm[FM_DK + idx // 4, 24 * (idx % 4):24 * (idx % 4) + 24, :], [], [kn])
            make_vp(Vp, vn, Vall, "Vall", 64 * hh)

        def diff_fin1(O, On, par, mm_):
            def fin():
                rz, rzn = next_rz()
                RECIP(rz[64:128], O[64:128], [On], [rzn])
                TT("dve", o12[par][mm_][0:64], O[0:64], rz[64:128], ALU.mult, [On, rzn], ["o12_%d%d" % (par, mm_)])
            return fin

        def diff_fin2(O, On, par, hh, qg):
            f1 = diff_fin1(O, On, par, 1)

            def fin():
                f1()
                dd, ddn = dds[par], "dd%d" % par
                sqd, sqn = sqds[par], "sqd%d" % par
                STT("dve", dd[0:64], o12[par][1][0:64], neglam[l][0:64, 0:1], o12[par][0][0:64], ALU.mult, ALU.add,
                    ["o12_%d0" % par, "o12_%d1" % par, "neglam%d" % l], [ddn])
                ACT(sqd[0:64], dd[0:64], AF.Square, [ddn], [sqn])

                def finb():
                    M_, Mn = next_m()
                    MM(M_[0:64], onesb[0:64, 0:64], sqd[0:64], True, True, [sqn, "onesb"], [Mn])
                    rz, rzn = next_rz()
                    ACT(rz[0:64], M_[0:64], AF.Sqrt, [Mn, "epsc"], [rzn], scale=1.0 / 64.0, bias=epsc[0:64, 0:1])
                    RECIP(rz[0:64], rz[0:64], [rzn], [rzn])
                    os_, osn = next_os()
                    STT("dve", os_[0:64], dd[0:64], gsub[l][0:64, 0:1], rz[0:64], ALU.mult, ALU.mult,
                        [ddn, "gsub%d" % l, rzn], [osn])
                    P.dma("sync", ot[6 + hh // 2, 64 * (hh % 2):64 * (hh % 2) + 64, qg * 512:(qg + 1) * 512], os_[0:64], [osn], [])
                pipe_push(finb)
            return fin

        diff_load(0)
        npar = 0
        for hh in range(4):
            if hh + 1 < 4:
                pipe_flush()
                diff_load(hh + 1)
            s = hh % 2
            for qg in range(NG):
                par = npar % 2
                npar += 1
                for mm_ in range(2):
                    O, On = next_o()
                    fin = diff_fin1(O, On, par, 0) if mm_ == 0 else diff_fin2(O, On, par, hh, qg)
                    dense_causal(kTs[s][mm_], "kT%d_%d" % (s, mm_), qTs[s], "qT%d" % s, 128, 0, Vps[s], "Vp%d" % s, sc_c, qg, O, On, fin=fin)
        pipe_flush()
        P.barrier()
        if os.environ.get("KSTOP") == "diff":
            raise StopBuild()

        A.reset()
        set_banks(3, 2, 2)
        CO, COn = pb[7], "pb7"
        cpt = [A.alloc([512]) for _ in range(2)]
        Emat = A.alloc([T])
        cmask = A.alloc([NCC, T])
        phik1_l = A.alloc([32, 64])
        phiv1_l = A.alloc([32, 64])
        MS("pool", Emat, 1.0, ["Emat"])
        ASEL(Emat, Emat, [[1, T]], ALU.is_ge, 0.0, 0, -64, ["Emat"], ["Emat"])
        ASEL(Emat, Emat, [[-1, T]], ALU.is_ge, 0.0, 63, 64, ["Emat"], ["Emat"])
        MS("pool", cmask, 1.0, ["cmask"])
        for cc in range(NCC):
            ASEL(cmask[:, cc, :], cmask[:, cc, :], [[1, T]], ALU.is_ge, 0.0, -31 - 2048 * cc, -16, ["cmask"], ["cmask"])
        P.dma("pool", phik1_l[0:64], W[l]["phik1"].rearrange("(l d) j -> d l j", d=64), (), ["phik1"])
        P.dma("pool", phiv1_l[0:64], W[l]["phiv1"].rearrange("(l d) j -> d l j", d=64), (), ["phiv1"])
        gT = A.alloc([T])
        MS("pool", gT, 0.0, ["gT"])
        P.dma("sync", gT[0:18], fm[FM_GT, 0:18, :], [], ["gT"])
        Gsel = A.alloc([18, 64])
        MS("pool", Gsel, 1.0, ["Gsel"])
        for j in range(18):
            ASEL(Gsel[:, j, :], Gsel[:, j, :], [[0, 64]], ALU.is_equal, 0.0, -j, 1, ["Gsel"], ["Gsel"])
        VallS = A.alloc([NT, 256])
        P.dma("sync", VallS, tmv[:, 0:256].rearrange("(tt p) c -> p tt c", p=128), [], ["VallS"])
        qTn = [A.alloc([T]) for _ in range(3)]
        ksT = A.alloc([T])
        kwT = A.alloc([T])
        kcT = A.alloc([T])
        vcT = A.alloc([T])
        VpS = A.alloc([NT, 128])
        VpW = A.alloc([NT, 128])
        MS("pool", VpS, 1.0, ["VpS"])
        MS("pool", VpW, 1.0, ["VpW"])
        for hg_ in range(3):
            MS("pool", qTn[hg_], 0.0, ["qTn%d" % hg_])
        MS("pool", ksT, 0.0, ["ksT"])
        CP("dve", ksT[64:128, :], Emat[0:64, :], ["Emat", "ksT"], ["ksT"])
        MS("pool", kwT, 0.0, ["kwT"])
        kcmpT = A.alloc([NCC * 128])
        VpC = A.alloc([NCC, 128])
        hk = A.alloc([NCC * 128])
        hv = A.alloc([NCC * 128])
        biask = A.alloc([1], F32)
        biasv = A.alloc([1], F32)
        negbTs = [A.alloc([512]) for _ in range(2)]
        for i_ in range(2):
            MS("pool", negbTs[i_], 0.0, ["negbT%d" % i_])
        score = A.alloc([4, 64], F32)
        scw = A.alloc([4, 64], F32)
        impsb = A.alloc([260], F32)
        FBpat = A.alloc([4, 10], F32)
        MS("dve", FBpat, 0.0, ["FBpat"])
        for tt_ in range(4):
            MS("dve", FBpat[0:64, tt_, 2 * tt_ + 1:2 * tt_ + 2], 20.0, ["FBpat"])
            MS("dve", FBpat[0:64, tt_, 2 * tt_:2 * tt_ + 1], 40.0, ["FBpat"])
            MS("dve", FBpat[64:128, tt_, 2 * tt_ + 2:2 * tt_ + 3], 20.0, ["FBpat"])
            MS("dve", FBpat[64:128, tt_, 2 * tt_ + 1:2 * tt_ + 2], 40.0, ["FBpat"])
        mx8 = A.alloc([4, 16], F32)
        rzc = A.alloc([4, 4], F32)
        nbf = A.alloc([4, 64], F32)
        ocgs = [[A.alloc([512], F32) for _ in range(3)] for _ in range(2)]
        gbs = [A.alloc([512], F32) for _ in range(3)]
        gcnt = [0]
        sc_b = 64.0 ** -0.5

        def next_gb():
            i = gcnt[0] % 3
            gcnt[0] += 1
            return gbs[i], "gb%d" % i

        for g in range(2):
            pipe_flush()
            for hg in range(3):
                h = 3 * g + hg
                P.dma("sync", qTn[hg][0:16], fm[FM_RA, 16 * h:16 * h + 16, :], [], ["qTn%d" % hg])
                P.dma("sync", qTn[hg][16:64], fm[FM_NQ + h // 2, 48 * (h % 2):48 * (h % 2) + 48, :], [], ["qTn%d" % hg])
            for (b, tl, tn) in ((0, kcT, "kcT"), (1, ksT, "ksT"), (2, kwT, "kwT")):
                P.dma("sync", tl[0:16], fm[FM_RB, 16 * (2 * b + g):16 * (2 * b + g) + 16, :], [], [tn])
                P.dma("sync", tl[16:64], fm[FM_NK + b, 48 * g:48 * g + 48, :], [], [tn])
            P.dma("sync", vcT[0:64], fm[FM_VC, 64 * g:64 * g + 64, :], [], ["vcT"])
            make_vp(VpS, "VpS", VallS, "VallS", 64 * g)
            make_vp(VpW, "VpW", VallS, "VallS", 128 + 64 * g)
            MS("pool", kcmpT, 0.0, ["kcmpT"])
            MS("pool", VpC, 0.0, ["VpC"])
            MS("pool", VpC[:, :, 64:128], 1.0, ["VpC"])
            MS("pool", hk, 0.0, ["hk"])
            MS("pool", hv, 0.0, ["hv"])
            for (srcT, srcn, w1, w1n, w2, w2n, peT, pen, hh_, hn, bias_, bn, isk) in (
                    (kcT, "kcT", phik1_l, "phik1", phik2_sb[l], "phik2%d" % l, pekT[l], "pekT%d" % l, hk, "hk", biask, "biask", True),
                    (vcT, "vcT", phiv1_l, "phiv1", phiv2_sb[l], "phiv2%d" % l, pevT[l], "pevT%d" % l, hv, "hv", biasv, "biasv", False)):
                Mb, Mbn = next_m()
                for li in range(32):
                    MM(Mb[0:64, 0:1], w1[0:64, li, :], peT[0:64, li:li + 1], li == 0, li == 31, [w1n, pen], [Mbn])
                CP("dve", bias_[0:64], Mb[0:64, 0:1], [Mbn], [bn])
                Mp, Mpn = next_m()
                for li in range(32):
                    MM(Mp[0:64, 0:NCMP], w1[0:64, li, :], srcT[0:64, li:li + 16 * (NCMP - 1) + 1:16], li == 0, li == 31, [w1n, srcn], [Mpn])
                ACT(hh_[0:64, 0:NCMP], Mp[0:64, 0:NCMP], AF.Silu, [Mpn, bn], [hn], bias=bias_[0:64, 0:1])
                if isk:
                    M2, M2n = next_m()
                    MM(M2[0:64, 0:NCC * 128], w2[0:64, :], hh_[0:64, :], True, True, [w2n, hn], [M2n])
                    CP("dve", kcmpT[0:64, 0:NCMP], M2[0:64, 0:NCMP], [M2n], ["kcmpT"])
                else:
                    for cc in range(NCC):
                        rows = min(128, NCMP - 128 * cc)
                        M2, M2n = next_m()
                        MM(M2[0:rows, 0:64], hh_[0:64, cc * 128:cc * 128 + rows], w2[0:64, :], True, True, [w2n, hn], [M2n])
                        CP("dve", VpC[0:rows, cc, 0:64], M2[0:rows, 0:64], [M2n], ["VpC"])

            def stage_a1(qg):
                pipe_flush()
                q0 = qg * 512
                par = qg % 2
                for hg in range(3):
                    h = 3 * g + hg
                    qT, qn = qTn[hg], "qTn%d" % hg
                    ccs = [cc for cc in range(NCC) if 16 * (128 * cc) + 31 <= q0 + 511]
                    O, On = CO, COn
                    ets = []
                    for ci, cc in enumerate(ccs):
                        S, Sn = next_m()
                        Pt, Ptn = cpt[ci], "cpt%d" % ci
                        MM(S, kcmpT[:, cc * 128:(cc + 1) * 128], qT[:, q0:q0 + 512], True, True, ["kcmpT", qn], [Sn])
                        ACT(Pt, S, AF.Exp, [Sn], [Ptn], scale=sc_b)
                        TT("dve", Pt, Pt, cmask[:, cc, q0:q0 + 512], ALU.mult, [Ptn, "cmask"], [Ptn])
                        ets.append((cc, Pt, Ptn))
                    for ci, (cc, Pt, Ptn) in enumerate(ets):
                        MM(O, VpC[:, cc, :], Pt, ci == 0, ci == len(ets) - 1, ["VpC", Ptn], [On])
                    M_, Mn = next_m()
                    Mv = M_[:, 0:260].rearrange("p (t c) -> p t c", c=65)
                    for tt in range(4):
                        for ci, (cc, Pt, Ptn) in enumerate(ets):
                            MM(M_[:, tt * 65:tt * 65 + 65], Pt[:, tt * 128:(tt + 1) * 128], OV[:, cc, :], ci == 0, ci == len(ets) - 1, [Ptn, "OV"], [Mn])
                    CP("dve", impsb, M_[:, 0:260], [Mn], ["impsb"])
                    Ms = impsb.rearrange("p (t c) -> p t c", c=65)
                    TS("dve", rzc[:, :, hg], Ms[:, :, 64], tinyc[:, 0:1], None, ALU.add, None, ["impsb", "tinyc"], ["rzc"])
                    RECIP(rzc[:, :, hg], rzc[:, :, hg], ["rzc"], ["rzc"])
                    rzb = rzc[:, :, hg:hg + 1].to_broadcast([128, 4, 64])
                    if hg == 0:
                        TT("dve", score, Ms[:, :, 0:64], rzb, ALU.mult, ["impsb", "rzc"], ["score"])
                    else:
                        TT("dve", scw, Ms[:, :, 0:64], rzb, ALU.mult, ["impsb", "rzc"], ["scw"])
                        TT("dve", score, score, scw, ALU.add, ["score", "scw"], ["score"])
                    rz, rzn = next_rz()
                    TS("dve", rz[64:128], O[64:128], tinyc[64:128, 0:1], None, ALU.add, None, [On, "tinyc"], [rzn])
                    RECIP(rz[64:128], rz[64:128], [rzn], [rzn])
                    Gp, Gpn = next_m()
                    MM(Gp[0:64], Gsel[:, 3 * h + 0, :], gT[:, q0:q0 + 512], True, True, ["Gsel", "gT"], [Gpn])
                    gb_, gbn = next_gb()
                    TT("dve", gb_[0:64], Gp[0:64], rz[64:128], ALU.mult, [Gpn, rzn], [gbn])
                    TT("dve", ocgs[par][hg][0:64], O[0:64], gb_[0:64], ALU.mult, [On, gbn], ["ocg%d%d" % (par, hg)])
                TS("dve", score[:, :, 0:1], score[:, :, 0:1], 10.0, None, ALU.add, None, ["score"], ["score"])
                if qg == 0:
                    TT("dve", score[:, :, 0:8], score[:, :, 0:8], FBpat[:, :, 1:9], ALU.add, ["score", "FBpat"], ["score"])
                else:
                    TT("dve", score[:, :, 8 * qg - 1:8 * qg + 8], score[:, :, 8 * qg - 1:8 * qg + 8], FBpat[:, :, 0:9], ALU.add,
                       ["score", "FBpat"], ["score"])
                for tt in range(4):
                    P.op("dve", lambda e, o=mx8[:, tt, 0:8], i=score[:, tt, 0:NSLC]: e.max(out=o, in_=i), ["score"], ["mx8"])
                    P.op("dve", lambda e, o=scw[:, tt, 0:NSLC], r=mx8[:, tt, 0:8], i=score[:, tt, 0:NSLC]:
                         e.match_replace(out=o, in_to_replace=r, in_values=i, imm_value=-1e30), ["score", "mx8"], ["scw"])
                    P.op("dve", lambda e, o=mx8[:, tt, 8:16], i=scw[:, tt, 0:NSLC]: e.max(out=o, in_=i), ["scw"], ["mx8"])
                    TS("dve", nbf[:, tt, 0:NSLC], score[:, tt, 0:NSLC], mx8[:, tt, 15:16], -30000.0, ALU.is_lt, ALU.mult, ["score", "mx8"], ["nbf"])

            def stage_a2(qg):
                par = qg % 2
                for tt in range(4):
                    M_, Mn = next_m()
                    P.op("pe", lambda e, o=M_[0:NSLC, 0:128], i=nbf[:, tt, 0:NSLC]: e.transpose(out=o, in_=i, identity=identf),
                         ["nbf", "identf"], [Mn])
                    for hg_ in range(3):
                        ACT(qTn[hg_][64:64 + NSLC, qg * 512 + tt * 128:qg * 512 + (tt + 1) * 128], M_[0:NSLC, 0:128], AF.Copy,
                            [Mn], ["qTn%d" % hg_])

            def mk_extra(par):
                def slc_extra(kc, S, Sn, c0, c1, last):
                    MM(S[:, c0:c1], Emat[:, kc * 128:(kc + 1) * 128], negbTs[par][:, c0:c1], False, last,
                       ["Emat", "negbT%d" % par], [Sn])
                return slc_extra

            def nsa_fin(O, On, h, hg, br, qg):
                par = qg % 2
                q0 = qg * 512

                def fin():
                    rz, rzn = next_rz()
                    RECIP(rz[64:128], O[64:128], [On], [rzn])
                    Gp, Gpn = next_m()
                    MM(Gp[0:64], Gsel[:, 3 * h + 1 + br, :], gT[:, q0:q0 + 512], True, True, ["Gsel", "gT"], [Gpn])
                    gb_, gbn = next_gb()
                    TT("dve", gb_[0:64], Gp[0:64], rz[64:128], ALU.mult, [Gpn, rzn], [gbn])
                    TT("dve", gb_[0:64], O[0:64], gb_[0:64], ALU.mult, [On, gbn], [gbn])
                    ocn = "ocg%d%d" % (par, hg)
                    if br == 0:
                        TT("dve", ocgs[par][hg][0:64], ocgs[par][hg][0:64], gb_[0:64], ALU.add, [ocn, gbn], [ocn])
                    else:
                        os_, osn = next_os()
                        TT("dve", os_[0:64], ocgs[par][hg][0:64], gb_[0:64], ALU.add, [ocn, gbn], [osn])
                        P.dma("sync", ot[3 + h // 2, 64 * (h % 2):64 * (h % 2) + 64, q0:q0 + 512], os_[0:64], [osn], [])
                return fin

            def stage_b(qg, mid):
                par = qg % 2
                for hg in range(3):
                    h = 3 * g + hg
                    qT, qn = qTn[hg], "qTn%d" % hg
                    for br, (kT_, kn_, Vp_, vn_) in enumerate(((ksT, "ksT", VpS, "VpS"), (kwT, "kwT", VpW, "VpW"))):
                        O, On = next_o()
                        dense_causal(kT_, kn_, qT, qn, 128, 0, Vp_, vn_, sc_b, qg, O, On,
                                     extra=None, win=(br == 1),
                                     fin=nsa_fin(O, On, h, hg, br, qg))
                    if hg == 0 and mid is not None:
                        mid()

            stage_a1(0)
            stage_a2(0)
            for qg in range(NG):
                if qg + 1 < NG:
                    stage_a1(qg + 1)
                    stage_b(qg, lambda q=qg + 1: stage_a2(q))
                else:
                    stage_b(qg, None)
        pipe_flush()
        P.barrier()

    try:
      if os.environ.get("KSTOP") == "setup":
          raise StopBuild()
      row_phase(0, 0)
      if os.environ.get("KSTOP") == "row0":
          raise StopBuild()
      for l in range(L):
        attn_phase(l)
        if os.environ.get("KSTOP") == "nsa":
            raise StopBuild()
        if dbg and l == 0:
            P.dma("sync", dbg_out["d_xres"], xres, [], [])
            P.dma("sync", dbg_out["d_fm"], fm, [], [])
            P.dma("sync", dbg_out["d_tmv"], tmv, [], [])
            P.dma("sync", dbg_out["d_ot"], ot, [], [])
            P.barrier()
        if l + 1 < L:
            row_phase(1, l + 1)
        else:
            row_phase(2, L)
    except StopBuild:
        pass
    P.emit(nc, st)
    st.close()
    return nc


def make_in_maps(inputs, T, L=DEPTH, batches=None):
    f32 = np.float32
    x = np.asarray(inputs["x"], f32)
    pos = np.asarray(inputs["positions"], np.int32)
    B = x.shape[0]
    wperm = _win_perm()
    qperm = _wuq_perm()
    kvperm = _wukv_perm()
    shared = {}
    shared["fnorm"] = np.ascontiguousarray(np.broadcast_to(np.asarray(inputs["final_norm"], f32)[None, :], (128, D_MODEL)))
    shared["ropef"] = _rope_consts()
    for l in range(L):
        g = lambda k: np.asarray(inputs[k][l], f32)
        shared["wg1_%d" % l] = np.ascontiguousarray(g("ffn1_wg"))
        shared["wu1_%d" % l] = np.ascontiguousarray(g("ffn1_wu"))
        shared["wd1_%d" % l] = np.ascontiguousarray(g("ffn1_wd"))
        shared["wg2_%d" % l] = np.ascontiguousarray(g("ffn2_wg"))
        shared["wu2_%d" % l] = np.ascontiguousarray(g("ffn2_wu"))
        shared["wd2_%d" % l] = np.ascontiguousarray(g("ffn2_wd"))
        shared["win_%d" % l] = _take_cols(g("w_in"), wperm)
        shared["wuq_%d" % l] = _take_cols(g("mla_w_uq"), qperm)
        shared["wukv_%d" % l] = _take_cols(g("mla_w_ukv"), kvperm)
        shared["wout_%d" % l] = np.ascontiguousarray(g("w_out"))
        shared["phik1_%d" % l] = np.ascontiguousarray(g("nsa_phi_k1"))
        shared["phiv1_%d" % l] = np.ascontiguousarray(g("nsa_phi_v1"))
        shared["phik2_%d" % l] = np.ascontiguousarray(g("nsa_phi_k2"))
        shared["phiv2_%d" % l] = np.ascontiguousarray(g("nsa_phi_v2"))
        sm = np.zeros((128, NSMALL), f32)
        sm[:, 0:8] = g("ffn1_norm").reshape(8, 128).T
        sm[:, 8:16] = g("mix_norm").reshape(8, 128).T
        sm[:, 16:24] = g("ffn2_norm").reshape(8, 128).T
        qn = g("mla_q_norm")
        sm[:, 24] = qn[0:128]
        sm[0:64, 25] = qn[128:192]
        sm[:, 26] = g("mla_kv_norm")
        sn = g("diff_sub_norm")
        sm[0:64, 27] = sn
        sm[64:128, 27] = sn
        sm[:, 28:60] = g("diff_lq1")[None, :]
        sm[:, 60:92] = g("diff_lk1")[None, :]
        sm[:, 92:124] = g("diff_lq2")[None, :]
        sm[:, 124:156] = g("diff_lk2")[None, :]
        sm[0:64, 156:188] = g("nsa_pe_k").T
        sm[0:64, 188:220] = g("nsa_pe_v").T
        shared["small_%d" % l] = sm
    maps = []
    for b in (batches if batches is not None else range(B)):
        m = dict(shared)
        m["x"] = np.ascontiguousarray(x[b])
        m["pos"] = np.ascontiguousarray(pos[b][None, :])
        maps.append(m)
    return maps


_NC_CACHE = {}


def kernel(**inputs):
    x = np.asarray(inputs["x"])
    B, T, _ = x.shape
    if T not in _NC_CACHE:
        _NC_CACHE[T] = build(T, DEPTH)
    nc = _NC_CACHE[T]
    maps = make_in_maps(inputs, T)
    res = run_bass_kernel_spmd(nc, maps, core_ids=list(range(B)))
    out = np.stack([np.asarray(r["y"]) for r in res.results], axis=0)
    return out.astype(np.float32)
```

```python
import math
import os
from contextlib import ExitStack
import numpy as np
import concourse.bass as bass
import concourse.mybir as mybir
from concourse.bass_utils import run_bass_kernel_spmd

F32 = mybir.dt.float32
BF16 = mybir.dt.bfloat16
I32 = mybir.dt.int32
AF = mybir.ActivationFunctionType
ALU = mybir.AluOpType
AX = mybir.AxisListType

D_MODEL = 1024
D_FF = 2816
DEPTH = 2
EPS = 1e-6
THETA = 500000.0
NFC = D_FF // 128
WIN_NCH = 25
WIN_COLS = WIN_NCH * 128
WUQ_COLS = 7 * 128
NSMALL = 220

ENGS = ("sync", "act", "dve", "pool", "pe")
NDMA_SEM = 8


class StopBuild(Exception):
    pass


class Op:
    __slots__ = ("eng", "fn", "deps", "dma", "needs_inc", "sem", "val", "idx")

    def __init__(self, eng, fn, dma):
        self.eng = eng
        self.fn = fn
        self.deps = []
        self.dma = dma
        self.needs_inc = dma
        self.sem = None
        self.val = 0
        self.idx = 0


class Prog:
    def __init__(self):
        self.ops = {e: [] for e in ENGS}
        self.last_writer = {}
        self.readers = {}

    def op(self, eng, fn, reads=(), writes=(), dma=False):
        o = Op(eng, fn, dma)
        deps = set()
        for r in reads:
            lw = self.last_writer.get(r)
            if lw is not None:
                deps.add(lw)
        for w in writes:
            lw = self.last_writer.get(w)
            if lw is not None:
                deps.add(lw)
            for rd in self.readers.get(w, ()):
                deps.add(rd)
        for d in deps:
            if eng == "pe" and d.eng == "pe" and not d.dma:
                continue
            o.deps.append(d)
            d.needs_inc = True
        for w in writes:
            self.last_writer[w] = o
            self.readers[w] = []
        for r in reads:
            if r not in writes:
                self.readers.setdefault(r, []).append(o)
        self.ops[eng].append(o)
        return o

    def dma(self, q, out, in_, reads=(), writes=(), **kw):
        return self.op(q, lambda e: e.dma_start(out=out, in_=in_, **kw), reads, writes, dma=True)

    def barrier(self, skip_dma_of=()):
        lasts = []
        for e in ENGS:
            ops = self.ops[e]
            for o in reversed(ops):
                if not o.dma:
                    lasts.append(o)
                    break
            nd = 0
            for o in reversed(ops):
                if o.dma and e not in skip_dma_of:
                    lasts.append(o)
                    nd += 1
                    if nd >= NDMA_SEM:
                        break
        for e in ENGS:
            o = Op(e, lambda eng: eng.nop(), False)
            for d in lasts:
                if e == "pe" and d.eng == "pe" and not d.dma:
                    continue
                o.deps.append(d)
                d.needs_inc = True
            self.ops[e].append(o)
        self.last_writer.clear()
        self.readers.clear()

    def emit(self, nc, stack):
        sems = {}
        dsems = {}
        for e in ENGS:
            sems[e] = stack.enter_context(nc.semaphore("s_" + e))
            if any(o.dma for o in self.ops[e]):
                dsems[e] = [stack.enter_context(nc.semaphore("d_%s%d" % (e, i))) for i in range(NDMA_SEM)]
        for e in ENGS:
            cnt = 0
            nd = 0
            for o in self.ops[e]:
                if o.dma:
                    o.sem = dsems[e][nd % NDMA_SEM]
                    o.val = 16 * (nd // NDMA_SEM + 1)
                    o.idx = nd
                    nd += 1
                elif o.needs_inc:
                    cnt += 1
                    o.sem = sems[e]
                    o.val = cnt
        all_ops = self.ops

        def run_engine(ename, eng):
            waited = {}
            dma_hist = []
            for o in all_ops[ename]:
                waits = {}
                for d in o.deps:
                    k = d.sem
                    if waits.get(id(k), (None, 0))[1] < d.val:
                        waits[id(k)] = (k, d.val)
                if o.dma:
                    if o.idx >= NDMA_SEM:
                        p = dma_hist[o.idx - NDMA_SEM]
                        if waits.get(id(p.sem), (None, 0))[1] < p.val:
                            waits[id(p.sem)] = (p.sem, p.val)
                    dma_hist.append(o)
                for kid, (k, v) in waits.items():
                    if waited.get(kid, 0) >= v:
                        continue
                    waited[kid] = v
                    eng.wait_ge(k, v)
                ins = o.fn(eng)
                if o.dma:
                    ins.then_inc(o.sem, 16)
                elif o.needs_inc:
                    ins.then_inc(o.sem, 1)
            for p in dma_hist[-NDMA_SEM:]:
                if waited.get(id(p.sem), 0) < p.val:
                    eng.wait_ge(p.sem, p.val)

        with nc.Block() as block:
            @block.sync
            def _(eng):
                run_engine("sync", eng)

            @block.scalar
            def _(eng):
                run_engine("act", eng)

            @block.vector
            def _(eng):
                run_engine("dve", eng)

            @block.gpsimd
            def _(eng):
                run_engine("pool", eng)

            @block.tensor
            def _(eng):
                run_engine("pe", eng)


class Arena:
    def __init__(self, handle, n):
        self.h = handle
        self.n = n
        self.off = 0
        self.mark = 0

    def set_mark(self):
        self.mark = self.off

    def reset(self):
        self.off = self.mark

    def alloc(self, free_shape, dt=BF16):
        nel = 1
        for s in free_shape:
            nel *= s
        nb = nel * (4 if dt in (F32, I32) else 2)
        nb = (nb + 63) // 64 * 64
        n16 = nb // 2
        assert self.off + n16 <= self.n, "SBUF arena overflow %d + %d > %d" % (self.off, n16, self.n)
        ap = self.h[:, self.off:self.off + n16]
        self.off += n16
        if dt != BF16:
            ap = ap.bitcast(dt)
        ap = ap[:, 0:nel]
        if len(free_shape) == 2:
            ap = ap.rearrange("p (a b) -> p a b", a=free_shape[0])
        elif len(free_shape) == 3:
            ap = ap.rearrange("p (a b c) -> p a b c", a=free_shape[0], b=free_shape[1])
        return ap


A_OFF, B_OFF, C_OFF = 0, 352, 1522


def _win_perm():
    ch = []

    def pad(lst):
        assert len(lst) <= 128
        return lst + [-1] * (128 - len(lst))

    ch.append(pad(list(range(0, 128))))
    ch.append(pad(list(range(128, 192))))
    ch.append(pad(list(range(192, 320))))
    ra, rap = [], []
    for h in range(6):
        for d in range(16):
            ra.append(B_OFF + 64 * h + d)
            rap.append(B_OFF + 64 * h + (d + 8) % 16)
    for d in range(32):
        ra.append(A_OFF + 320 + d)
        rap.append(A_OFF + 320 + (d + 16) % 32)
    ch.append(pad(ra))
    ch.append(pad(rap))
    rb, rbp = [], []
    for b in range(3):
        for g in range(2):
            base = B_OFF + 384 + 256 * b + 64 * g
            for d in range(16):
                rb.append(base + d)
                rbp.append(base + (d + 8) % 16)
    ch.append(pad(rb))
    ch.append(pad(rbp))
    rc, rcp = [], []
    for qk in range(2):
        for idx in range(8):
            base = C_OFF + 256 * qk + 32 * idx
            for d in range(8):
                rc.append(base + d)
                rcp.append(base + (d + 4) % 8)
    ch.append(pad(rc))
    ch.append(pad(rcp))
    for pr in range(3):
        l = []
        for h in (2 * pr, 2 * pr + 1):
            l += [B_OFF + 64 * h + d for d in range(16, 64)]
        ch.append(pad(l))
    for b in range(3):
        l = []
        for g in range(2):
            base = B_OFF + 384 + 256 * b + 64 * g
            l += [base + d for d in range(16, 64)]
        ch.append(pad(l))
    ch.append(pad([B_OFF + 512 + d for d in range(128)]))
    for qk in range(2):
        for half in range(2):
            l = []
            for idx in range(4 * half, 4 * half + 4):
                base = C_OFF + 256 * qk + 32 * idx
                l += [base + d for d in range(8, 32)]
            ch.append(pad(l))
    ch.append(pad([B_OFF + 1152 + d for d in range(18)]))
    ch.append(pad([B_OFF + 768 + d for d in range(128)]))
    ch.append(pad([B_OFF + 1024 + d for d in range(128)]))
    ch.append(pad([C_OFF + 512 + d for d in range(128)]))
    ch.append(pad([C_OFF + 640 + d for d in range(128)]))
    assert len(ch) == WIN_NCH
    return [c for l in ch for c in l]


def _wuq_perm():
    ch = []
    for pr in range(3):
        l = []
        for h in (2 * pr, 2 * pr + 1):
            l += [96 * h + d for d in range(64)]
        ch.append(l)
    r0, r0p, r1, r1p = [], [], [], []
    for h in range(4):
        for d in range(32):
            r0.append(96 * h + 64 + d)
            r0p.append(96 * h + 64 + (d + 16) % 32)
    for h in range(4, 6):
        for d in range(32):
            r1.append(96 * h + 64 + d)
            r1p.append(96 * h + 64 + (d + 16) % 32)
    r1 += [-1] * 64
    r1p += [-1] * 64
    ch += [r0, r0p, r1, r1p]
    return [c for l in ch for c in l]


def _wukv_perm():
    l = []
    for h in range(6):
        l += [128 * h + d for d in range(64)]
    for h in range(6):
        l += [128 * h + 64 + d for d in range(64)]
    return l


def _take_cols(w, perm):
    perm = np.asarray(perm)
    out = np.zeros((w.shape[0], len(perm)), dtype=w.dtype)
    m = perm >= 0
    out[:, m] = w[:, perm[m]]
    return out


def _rope_consts():
    def invf(rot):
        return (1.0 / (THETA ** (np.arange(0, rot, 2, dtype=np.float32) / np.float32(rot)))).astype(np.float32)
    fa, fb, fc = invf(32), invf(16), invf(8)
    out = np.zeros((128, 10), np.float32)

    def fill(rc, rows):
        for p, (f, sgn) in enumerate(rows):
            out[p, 2 * rc] = f
            out[p, 2 * rc + 1] = sgn * f
    ra = []
    for h in range(6):
        for d in range(16):
            ra.append((fb[d % 8], -1.0 if d < 8 else 1.0))
    for d in range(32):
        ra.append((fa[d % 16], -1.0 if d < 16 else 1.0))
    fill(0, ra)
    rb = []
    for _ in range(6):
        for d in range(16):
            rb.append((fb[d % 8], -1.0 if d < 8 else 1.0))
    fill(1, rb)
    rcl = []
    for _ in range(16):
        for d in range(8):
            rcl.append((fc[d % 4], -1.0 if d < 4 else 1.0))
    fill(2, rcl)
    q0 = []
    for _ in range(4):
        for d in range(32):
            q0.append((fa[d % 16], -1.0 if d < 16 else 1.0))
    fill(3, q0)
    fill(4, q0[:64])
    return out


def build(T, L=DEPTH, dbg=False):
    NG = T // 512
    NT = T // 128
    NCMP = T // 16 - 1
    NCC = (NCMP + 127) // 128
    NSLC = T // 64
    nc = bass.Bass("TRN2", target_bir_lowering=False)
    P = Prog()
    st = ExitStack()

    def din(name, shape, dt=F32):
        return nc.dram_tensor(name, list(shape), dt, kind="ExternalInput").ap()

    def dscr(name, shape, dt=BF16):
        return nc.dram_tensor(name, list(shape), dt).ap()

    x_in = din("x", [T, D_MODEL])
    pos_in = din("pos", [1, T], I32)
    fnorm_in = din("fnorm", [128, D_MODEL])
    ropef_in = din("ropef", [128, 10])
    W = []
    for l in range(L):
        d = {}
        for nm in ("wg1", "wu1", "wg2", "wu2"):
            d[nm] = din("%s_%d" % (nm, l), [D_MODEL, D_FF])
        for nm in ("wd1", "wd2"):
            d[nm] = din("%s_%d" % (nm, l), [D_FF, D_MODEL])
        d["win"] = din("win_%d" % l, [D_MODEL, WIN_COLS])
        d["wuq"] = din("wuq_%d" % l, [192, WUQ_COLS])
        d["wukv"] = din("wukv_%d" % l, [128, 768])
        d["wout"] = din("wout_%d" % l, [D_MODEL, D_MODEL])
        d["phik1"] = din("phik1_%d" % l, [2048, 64])
        d["phiv1"] = din("phiv1_%d" % l, [2048, 64])
        d["phik2"] = din("phik2_%d" % l, [64, 64])
        d["phiv2"] = din("phiv2_%d" % l, [64, 64])
        d["small"] = din("small_%d" % l, [128, NSMALL])
        W.append(d)
    y_out = nc.dram_tensor("y", [T, D_MODEL], F32, kind="ExternalOutput").ap()

    WS = []
    for l in range(L):
        d = {}
        for nm in ("wg1", "wu1", "wg2", "wu2"):
            d[nm] = dscr("s_%s_%d" % (nm, l), [D_MODEL, D_FF])
        for nm in ("wd1", "wd2"):
            d[nm] = dscr("s_%s_%d" % (nm, l), [D_FF, D_MODEL])
        d["win"] = dscr("s_win_%d" % l, [D_MODEL, WIN_COLS])
        d["wout"] = dscr("s_wout_%d" % l, [D_MODEL, D_MODEL])
        WS.append(d)
    xres = dscr("xres", [T, D_MODEL], F32)
    NFM = 23
    fm = dscr("fm", [NFM, 128, T])
    tmv = dscr("tmv", [T, 896])
    ot = dscr("ot", [8, 128, T])
    ropeT = dscr("ropeT", [10, 128, T], F32)
    dbg_out = {}
    if dbg:
        dbg_out["d_xres"] = nc.dram_tensor("d_xres", [T, D_MODEL], F32, kind="ExternalOutput").ap()
        dbg_out["d_fm"] = nc.dram_tensor("d_fm", [NFM, 128, T], BF16, kind="ExternalOutput").ap()
        dbg_out["d_tmv"] = nc.dram_tensor("d_tmv", [T, 896], BF16, kind="ExternalOutput").ap()
        dbg_out["d_ot"] = nc.dram_tensor("d_ot", [8, 128, T], BF16, kind="ExternalOutput").ap()

    FM_RA, FM_RB, FM_RC = 0, 1, 2
    FM_NQ, FM_NK, FM_VC, FM_DQ, FM_DK, FM_GT = 3, 6, 9, 10, 12, 14
    FM_QN, FM_QR, FM_KN = 15, 18, 20

    ARENA_N = 103 * 1024
    arena_h = nc.alloc_sbuf_tensor("arena", [128, ARENA_N], BF16)
    A = Arena(arena_h, ARENA_N)
    pb = [nc.alloc_psum_tensor("pb%d" % i, [128, 512], F32)[:] for i in range(8)]

    def MM(out, lhsT, rhs, start, stop, rd, wr):
        P.op("pe", lambda e: e.matmul(out, lhsT=lhsT, rhs=rhs, start=start, stop=stop), rd, wr)

    def ACT(out, in_, func, rd, wr, **kw):
        P.op("act", lambda e: e.activation(out=out, in_=in_, func=func, **kw), rd, wr)

    def TT(eng, out, in0, in1, op, rd, wr):
        P.op(eng, lambda e: e.tensor_tensor(out=out, in0=in0, in1=in1, op=op), rd, wr)

    def TS(eng, out, in0, s1, s2, op0, op1, rd, wr):
        if op1 is None:
            P.op(eng, lambda e: e.tensor_scalar(out=out, in0=in0, scalar1=s1, scalar2=None, op0=op0), rd, wr)
        else:
            P.op(eng, lambda e: e.tensor_scalar(out=out, in0=in0, scalar1=s1, scalar2=s2, op0=op0, op1=op1), rd, wr)

    def STT(eng, out, in0, scalar, in1, op0, op1, rd, wr):
        P.op(eng, lambda e: e.scalar_tensor_tensor(out=out, in0=in0, scalar=scalar, in1=in1, op0=op0, op1=op1), rd, wr)

    def CP(eng, out, in_, rd, wr):
        P.op(eng, lambda e: e.tensor_copy(out=out, in_=in_), rd, wr)

    def MS(eng, out, val, wr):
        P.op(eng, lambda e: e.memset(out, val), (), wr)

    def ASEL(out, in_, pattern, op, fill, base, cm, rd, wr):
        P.op("pool", lambda e: e.affine_select(out=out, in_=in_, pattern=pattern, compare_op=op, fill=fill,
                                                base=base, channel_multiplier=cm), rd, wr)

    def RECIP(out, in_, rd, wr):
        P.op("dve", lambda e: e.reciprocal(out=out, in_=in_), rd, wr)

    identf = A.alloc([128], F32)
    ident = A.alloc([128])
    onesb = A.alloc([128])
    tri = A.alloc([128])
    triw = A.alloc([128])
    ntri = A.alloc([128])
    ntriw = A.alloc([128])
    OV = A.alloc([NCC, 65])
    fnorm = A.alloc([D_MODEL], F32)
    ropef = A.alloc([10], F32)
    epsc = A.alloc([1], F32)
    tinyc = A.alloc([1], F32)
    smalls = [A.alloc([NSMALL], F32) for _ in range(L)]
    neglam = [A.alloc([1], F32) for _ in range(L)]
    gsub = [A.alloc([1], F32) for _ in range(L)]
    wuq_sb = [A.alloc([2, WUQ_COLS]) for _ in range(L)]
    wukv_sb = [A.alloc([768]) for _ in range(L)]
    phik2_sb = [A.alloc([64]) for _ in range(L)]
    phiv2_sb = [A.alloc([64]) for _ in range(L)]
    pekT = [A.alloc([32]) for _ in range(L)]
    pevT = [A.alloc([32]) for _ in range(L)]
    A.set_mark()

    MS("pool", identf, 1.0, ["identf"])
    ASEL(identf, identf, [[-1, 128]], ALU.is_equal, 0.0, 0, 1, ["identf"], ["identf"])
    CP("dve", ident, identf, ["identf"], ["ident"])
    MS("pool", onesb, 1.0, ["onesb"])
    MS("pool", tri, 1.0, ["tri"])
    ASEL(tri, tri, [[1, 128]], ALU.is_ge, 0.0, 0, -1, ["tri"], ["tri"])
    MS("pool", triw, 1.0, ["triw"])
    ASEL(triw, triw, [[-1, 128]], ALU.is_gt, 0.0, 0, 1, ["triw"], ["triw"])
    MS("pool", ntri, -30000.0, ["ntri"])
    ASEL(ntri, ntri, [[-1, 128]], ALU.is_gt, 0.0, 0, 1, ["ntri"], ["ntri"])
    MS("pool", ntriw, -30000.0, ["ntriw"])
    ASEL(ntriw, ntriw, [[1, 128]], ALU.is_ge, 0.0, 0, -1, ["ntriw"], ["ntriw"])
    MS("pool", OV, 1.0, ["OV"])
    for cc in range(NCC):
        ASEL(OV[:, cc, 0:64], OV[:, cc, 0:64], [[-4, 64]], ALU.is_ge, 0.0, 1 + 128 * cc, 1, ["OV"], ["OV"])
        ASEL(OV[:, cc, 0:64], OV[:, cc, 0:64], [[4, 64]], ALU.is_ge, 0.0, 3 - 128 * cc, -1, ["OV"], ["OV"])
    MS("dve", epsc, EPS, ["epsc"])
    MS("dve", tinyc, 1e-30, ["tinyc"])
    P.dma("sync", fnorm, fnorm_in, (), ["fnorm"])
    P.dma("sync", ropef, ropef_in, (), ["ropef"])
    for l in range(L):
        P.dma("sync", smalls[l], W[l]["small"], (), ["small%d" % l])
        P.dma("pool", wuq_sb[l][:, 0, :], W[l]["wuq"][0:128, :], (), ["wuq%d" % l])
        P.dma("pool", wuq_sb[l][0:64, 1, :], W[l]["wuq"][128:192, :], (), ["wuq%d" % l])
        P.dma("pool", wukv_sb[l], W[l]["wukv"], (), ["wukv%d" % l])
        P.dma("pool", phik2_sb[l][0:64], W[l]["phik2"], (), ["phik2%d" % l])
        P.dma("pool", phiv2_sb[l][0:64], W[l]["phiv2"], (), ["phiv2%d" % l])
        sm = smalls[l]
        CP("dve", pekT[l][0:64], sm[0:64, 156:188], ["small%d" % l], ["pekT%d" % l])
        CP("dve", pevT[l][0:64], sm[0:64, 188:220], ["small%d" % l], ["pevT%d" % l])
        lam_init = 0.8 - 0.6 * math.exp(-0.3 * l)
        tmp = A.alloc([32], F32)
        s12 = A.alloc([2], F32)
        TT("dve", tmp, sm[:, 28:60], sm[:, 60:92], ALU.mult, ["small%d" % l], ["lamtmp"])
        P.op("dve", lambda e, o=s12[:, 0:1], i=tmp: e.reduce_sum(out=o, in_=i, axis=AX.X), ["lamtmp"], ["s12a"])
        TT("dve", tmp, sm[:, 92:124], sm[:, 124:156], ALU.mult, ["small%d" % l, "s12a"], ["lamtmp"])
        P.op("dve", lambda e, o=s12[:, 1:2], i=tmp: e.reduce_sum(out=o, in_=i, axis=AX.X), ["lamtmp"], ["s12b"])
        ACT(s12, s12, AF.Exp, ["s12a", "s12b"], ["s12a", "s12b"])
        TT("dve", neglam[l], s12[:, 1:2], s12[:, 0:1], ALU.subtract, ["s12a", "s12b"], ["neglam%d" % l])
        TS("dve", neglam[l], neglam[l], -lam_init, None, ALU.add, None, ["neglam%d" % l], ["neglam%d" % l])
        TS("dve", gsub[l], sm[:, 27:28], 1.0 - lam_init, None, ALU.mult, None, ["small%d" % l], ["gsub%d" % l])
    A.set_mark()

    posi = A.alloc([T], I32)
    posf = A.alloc([T], F32)
    P.dma("sync", posi, pos_in.partition_broadcast(128), (), ["posi"])
    CP("dve", posf, posi, ["posi"], ["posf"])
    ra_ = [A.alloc([512], F32) for _ in range(2)]
    rk_ = [A.alloc([512], F32) for _ in range(2)]
    ri_ = [A.alloc([512], I32) for _ in range(2)]
    n = 0
    for tab in range(10):
        for g in range(NG):
            s = n % 2
            eng = "dve" if n % 2 == 0 else "pool"
            eng = "dve"
            n += 1
            a, k, ki = ra_[s], rk_[s], ri_[s]
            an, kn, kin = "ra%d" % s, "rk%d" % s, "ri%d" % s
            cs = slice(g * 512, (g + 1) * 512)
            if tab % 2 == 0:
                TS(eng, a, posf[:, cs], ropef[:, tab:tab + 1], math.pi / 2, ALU.mult, ALU.add, ["posf", "ropef"], [an])
            else:
                TS(eng, a, posf[:, cs], ropef[:, tab:tab + 1], None, ALU.mult, None, ["posf", "ropef"], [an])
            TS(eng, k, a, 1.0 / (2 * math.pi), None, ALU.mult, None, [an], [kn])
            CP(eng, ki, k, [kn], [kin])
            CP(eng, k, ki, [kin], [kn])
            STT(eng, a, k, -2 * math.pi, a, ALU.mult, ALU.add, [kn, an], [an])
            TS(eng, a, a, math.pi, -math.pi, ALU.min, ALU.max, [an], [an])
            ACT(a, a, AF.Sin, [an], [an])
            P.dma("sync", ropeT[tab, :, cs], a, [an], [])

    conv_cnt = [0]

    def convert_weights(stg_, items):
        for (l_, nm) in items:
            src = W[l_][nm]
            dst = WS[l_][nm]
            R, C = src.shape
            for r in range(R // 128):
                s = conv_cnt[0] % 3
                conv_cnt[0] += 1
                P.dma("pool", stg_[s][:, 0:C], src[r * 128:(r + 1) * 128, :], (), ["stg%d" % s])
                P.dma("pool", dst[r * 128:(r + 1) * 128, :], stg_[s][:, 0:C], ["stg%d" % s], [])

    stg = [A.alloc([WIN_COLS]) for _ in range(3)]
    convert_weights(stg, [(0, "wg1"), (0, "wu1"), (0, "wd1"), (0, "win")])
    P.barrier()
    A.reset()

    def row_phase(kind, l):
        A.reset()
        xs = [A.alloc([4, D_MODEL], F32) for _ in range(1)]
        xn = A.alloc([4, D_MODEL])
        xnT = A.alloc([8, 512])
        actT = A.alloc([NFC, 512])
        wslot = [A.alloc([8, 512]) for _ in range(3)]
        wd_sb = A.alloc([NFC, D_MODEL])
        wout_sb = A.alloc([8, D_MODEL]) if kind > 0 else None
        oin = A.alloc([8, 512]) if kind > 0 else None
        stage = [A.alloc([512]) for _ in range(4)]
        ss4 = A.alloc([4], F32)
        sg = [A.alloc([512], F32) for _ in range(2)]
        ftmp = [A.alloc([512], F32) for _ in range(4)]
        sqb = [A.alloc([512]) for _ in range(2)]
        cqn = A.alloc([2, 512])
        ckvn = A.alloc([512])
        ropeC = [A.alloc([512], F32) for _ in range(2)]
        ropeS = [A.alloc([512], F32) for _ in range(2)]
        cnt = {"ws": 0, "st": 0, "ft": 0, "pb": 0, "rp": 0}

        def next_pb():
            i = cnt["pb"] % 6
            cnt["pb"] += 1
            return pb[i], "pb%d" % i

        def next_stage():
            i = cnt["st"] % 4
            cnt["st"] += 1
            return stage[i], "stage%d" % i

        def next_ft():
            i = cnt["ft"] % 4
            cnt["ft"] += 1
            return ftmp[i], "ftmp%d" % i

        def load_wslot(src_ap_list):
            i = cnt["ws"] % 3
            cnt["ws"] += 1
            nm = "wslot%d" % i
            for (src, c0, rname) in src_ap_list:
                ncols = src.shape[1]
                P.dma("sync", wslot[i][:, :, c0:c0 + ncols], src.rearrange("(kc p) f -> p kc f", p=128), [], [nm])
            return wslot[i], nm

        def rmsnorm_T(xt, xname, gcol, gname):
            for tt in range(4):
                xr, xnr, sr = "%s_%d" % (xname, tt), "xn_%d" % tt, "ss4_%d" % tt
                ACT(xn[:, tt, :], xt[:, tt, :], AF.Square, [xr], [xnr, sr], accum_out=ss4[:, tt:tt + 1])
                ACT(ss4[:, tt:tt + 1], ss4[:, tt:tt + 1], AF.Sqrt, [sr, "epsc"], [sr], scale=1.0 / D_MODEL, bias=epsc[:, 0:1])
                RECIP(ss4[:, tt:tt + 1], ss4[:, tt:tt + 1], [sr], [sr])
                ACT(xn[:, tt, :], xt[:, tt, :], AF.Copy, [xr, sr], [xnr], scale=ss4[:, tt:tt + 1])
            for tt in range(4):
                pt_ = pb[6 + tt % 2][:, :].bitcast(BF16).rearrange("p (a b) -> p a b", a=8)
                pn = "pb%d" % (6 + tt % 2)
                for kc in range(8):
                    P.op("pe", lambda e, o=pt_[:, kc, :], i=xn[:, tt, kc * 128:(kc + 1) * 128]: e.transpose(out=o, in_=i, identity=ident),
                         ["xn_%d" % tt, "ident"], [pn])
                TT("dve", xnT[:, :, tt * 128:(tt + 1) * 128], pt_, gcol.unsqueeze(2).to_broadcast([128, 8, 128]),
                   ALU.mult, [pn, gname], ["xnT"])

        def ffn(xt, xname, wsd, l_, which):
            sm = smalls[l_]
            gcol = sm[:, 0:8] if which == 1 else sm[:, 16:24]
            wg, wu, wd = wsd["wg%d" % which], wsd["wu%d" % which], wsd["wd%d" % which]
            rmsnorm_T(xt, xname, gcol, "small%d" % l_)
            rg, ru, rd_ = "ws_wg%d_%d" % (which, l_), "ws_wu%d_%d" % (which, l_), "ws_wd%d_%d" % (which, l_)
            for fc2 in range(NFC // 2):
                wt, wn = load_wslot([(wg[:, fc2 * 256:(fc2 + 1) * 256], 0, rg), (wu[:, fc2 * 256:(fc2 + 1) * 256], 256, ru)])
                if fc2 == 2:
                    P.dma("sync", wd_sb, wd.rearrange("(fc p) d -> p fc d", p=128), [], ["wd_sb"])
                for sub in range(2):
                    fc = 2 * fc2 + sub
                    gp, gpn = next_pb()
                    up, upn = next_pb()
                    for kc in range(8):
                        MM(gp, wt[:, kc, sub * 128:(sub + 1) * 128], xnT[:, kc, :], kc == 0, kc == 7, [wn, "xnT"], [gpn])
                    for kc in range(8):
                        MM(up, wt[:, kc, 256 + sub * 128:256 + (sub + 1) * 128], xnT[:, kc, :], kc == 0, kc == 7, [wn, "xnT"], [upn])
                    s_ = sg[fc % 2]
                    sn = "sg%d" % (fc % 2)
                    ACT(s_, gp, AF.Silu, [gpn], [sn])
                    TT("dve", actT[:, fc, :], up, s_, ALU.mult, [upn, sn], ["actT%d" % fc])
            for tt in range(4):
                for dh in range(2):
                    yp, ypn = next_pb()
                    for fc in range(NFC):
                        MM(yp, actT[:, fc, tt * 128:(tt + 1) * 128], wd_sb[:, fc, dh * 512:(dh + 1) * 512], fc == 0, fc == NFC - 1,
                           ["actT%d" % fc, "wd_sb"], [ypn])
                    xsl = xt[:, tt, dh * 512:(dh + 1) * 512]
                    STT("dve", xsl, yp, 0.5, xsl, ALU.mult, ALU.add, [ypn, "%s_%d" % (xname, tt)], ["%s_%d" % (xname, tt)])

        def load_rope(rc, g):
            i = cnt["rp"] % 2
            cnt["rp"] += 1
            cs = slice(g * 512, (g + 1) * 512)
            P.dma("sync", ropeC[i], ropeT[2 * rc, :, cs], [], ["ropeC%d" % i])
            P.dma("sync", ropeS[i], ropeT[2 * rc + 1, :, cs], [], ["ropeS%d" % i])
            return ropeC[i], "ropeC%d" % i, ropeS[i], "ropeS%d" % i

        def rope_evac(mp, mpn, pp, ppn, rc, g, rows, fmid):
            C_, cn, S_, sn = load_rope(rc, g)
            t1, t1n = next_ft()
            t2, t2n = next_ft()
            so, son = next_stage()
            TT("dve", t1[0:rows], mp[0:rows], C_[0:rows], ALU.mult, [mpn, cn], [t1n])
            TT("dve", t2[0:rows], pp[0:rows], S_[0:rows], ALU.mult, [ppn, sn], [t2n])
            TT("dve", so[0:rows], t1[0:rows], t2[0:rows], ALU.add, [t1n, t2n], [son])
            P.dma("sync", fm[fmid, 0:rows, g * 512:(g + 1) * 512], so[0:rows], [son], [])

        def plain_evac(mp, mpn, rows, fmid, g, func=AF.Copy):
            so, son = next_stage()
            ACT(so[0:rows], mp[0:rows], func, [mpn], [son])
            P.dma("sync", fm[fmid, 0:rows, g * 512:(g + 1) * 512], so[0:rows], [son], [])

        def feat_rms(pslist, rowslist, gcols, gname, N, outs, outname):
            ssp, sspn = next_pb()
            for i, ((p_, pn_), rows) in enumerate(zip(pslist, rowslist)):
                sq = sqb[i]
                ACT(sq[0:rows], p_[0:rows], AF.Square, [pn_], ["sqb%d" % i])
                MM(ssp, onesb[0:rows, :], sq[0:rows], i == 0, i == len(pslist) - 1, ["sqb%d" % i, "onesb"], [sspn])
            rs, rsn = next_ft()
            ACT(rs, ssp, AF.Sqrt, [sspn, "epsc"], [rsn], scale=1.0 / N, bias=epsc[:, 0:1])
            RECIP(rs, rs, [rsn], [rsn])
            for i, ((p_, pn_), rows) in enumerate(zip(pslist, rowslist)):
                STT("dve", outs[i][0:rows], p_[0:rows], gcols[i][0:rows], rs[0:rows], ALU.mult, ALU.mult,
                    [pn_, gname, rsn], [outname])

        def win_step(xt, xname, l_, g, after_norm=None):
            sm = smalls[l_]
            wsrc = WS[l_]["win"]
            wr = "ws_win_%d" % l_
            slots = {}

            def ensure_slot(si):
                if si not in slots and si * 512 < WIN_COLS:
                    c0 = si * 512
                    c1 = min(WIN_COLS, c0 + 512)
                    slots[si] = load_wslot([(wsrc[:, c0:c1], 0, wr)])

            ensure_slot(0)
            ensure_slot(1)
            rmsnorm_T(xt, xname, sm[:, 8:16], "small%d" % l_)
            if after_norm is not None:
                after_norm()

            def chunk_w(ci):
                si = ci // 4
                ensure_slot(si)
                ensure_slot(si + 1)
                wt, wn = slots[si]
                return wt[:, :, (ci % 4) * 128:(ci % 4 + 1) * 128], wn

            def proj(ci, rows=128):
                wv, wn = chunk_w(ci)
                p_, pn_ = next_pb()
                for kc in range(8):
                    MM(p_[0:rows], wv[:, kc, 0:rows], xnT[:, kc, :], kc == 0, kc == 7, [wn, "xnT"], [pn_])
                return p_, pn_

            p0 = proj(0)
            p1 = proj(1, 64)
            feat_rms([p0, p1], [128, 64], [sm[:, 24:25], sm[:, 25:26]], "small%d" % l_, 192.0,
                     [cqn[:, 0, :], cqn[:, 1, :]], "cqn")
            p2 = proj(2)
            feat_rms([p2], [128], [sm[:, 26:27]], "small%d" % l_, 128.0, [ckvn], "ckvn")
            wq = wuq_sb[l_]
            wqn = "wuq%d" % l_

            def qproj(ci, rows=128):
                p_, pn_ = next_pb()
                MM(p_[0:rows], wq[:, 0, ci * 128:ci * 128 + rows], cqn[:, 0, :], True, False, [wqn, "cqn"], [pn_])
                MM(p_[0:rows], wq[0:64, 1, ci * 128:ci * 128 + rows], cqn[0:64, 1, :], False, True, [wqn, "cqn"], [pn_])
                return p_, pn_
            for pr in range(3):
                p_, pn_ = qproj(pr)
                plain_evac(p_, pn_, 128, FM_QN + pr, g)
            m_, mn_ = qproj(3)
            pp_, ppn_ = qproj(4)
            rope_evac(m_, mn_, pp_, ppn_, 3, g, 128, FM_QR)
            m_, mn_ = qproj(5, 64)
            pp_, ppn_ = qproj(6, 64)
            rope_evac(m_, mn_, pp_, ppn_, 4, g, 64, FM_QR + 1)
            wkv = wukv_sb[l_]
            wkvn = "wukv%d" % l_
            for pr in range(3):
                p_, pn_ = next_pb()
                MM(p_, wkv[:, pr * 128:(pr + 1) * 128], ckvn, True, True, [wkvn, "ckvn"], [pn_])
                plain_evac(p_, pn_, 128, FM_KN + pr, g)
            for tt in range(4):
                p_, pn_ = next_pb()
                MM(p_[:, 0:384], ckvn[:, tt * 128:(tt + 1) * 128], wkv[:, 384:768], True, True, [wkvn, "ckvn"], [pn_])
                so, son = next_stage()
                ACT(so[:, 0:384], p_[:, 0:384], AF.Copy, [pn_], [son])
                r0 = g * 512 + tt * 128
                P.dma("sync", tmv[r0:r0 + 128, 512:896], so[:, 0:384], [son], [])
            for (cm, cp_, rc, rows, fmid) in ((3, 4, 0, 128, FM_RA), (5, 6, 1, 96, FM_RB), (7, 8, 2, 128, FM_RC)):
                m_, mn_ = proj(cm, rows)
                pp_, ppn_ = proj(cp_, rows)
                rope_evac(m_, mn_, pp_, ppn_, rc, g, rows, fmid)
            for (ci, rows, fmid) in ((9, 96, FM_NQ), (10, 96, FM_NQ + 1), (11, 96, FM_NQ + 2),
                                     (12, 96, FM_NK), (13, 96, FM_NK + 1), (14, 96, FM_NK + 2),
                                     (15, 128, FM_VC), (16, 96, FM_DQ), (17, 96, FM_DQ + 1),
                                     (18, 96, FM_DK), (19, 96, FM_DK + 1)):
                p_, pn_ = proj(ci, rows)
                plain_evac(p_, pn_, rows, fmid, g)
            p_, pn_ = proj(20, 18)
            plain_evac(p_, pn_, 18, FM_GT, g, AF.Sigmoid)
            chunk_w(21)
            wt, wn = slots[21 // 4]
            assert 21 % 4 == 1
            chunk_w(24)
            wt2, wn2 = slots[6]
            for tt in range(4):
                p_, pn_ = next_pb()
                for kc in range(8):
                    MM(p_[:, 0:384], xnT[:, kc, tt * 128:(tt + 1) * 128], wt[:, kc, 128:512], kc == 0, kc == 7, [wn, "xnT"], [pn_])
                for kc in range(8):
                    MM(p_[:, 384:512], xnT[:, kc, tt * 128:(tt + 1) * 128], wt2[:, kc, 0:128], kc == 0, kc == 7, [wn2, "xnT"], [pn_])
                so, son = next_stage()
                ACT(so, p_, AF.Copy, [pn_], [son])
                r0 = g * 512 + tt * 128
                P.dma("sync", tmv[r0:r0 + 128, 0:512], so, [son], [])

        def wout_step(xt, xname, l_, g):
            for tt in range(4):
                for dh in range(2):
                    yp, ypn = next_pb()
                    for kc in range(8):
                        MM(yp, oin[:, kc, tt * 128:(tt + 1) * 128], wout_sb[:, kc, dh * 512:(dh + 1) * 512], kc == 0, kc == 7,
                           ["oin", "wout_sb"], [ypn])
                    xsl = xt[:, tt, dh * 512:(dh + 1) * 512]
                    TT("dve", xsl, yp, xsl, ALU.add, [ypn, "%s_%d" % (xname, tt)], ["%s_%d" % (xname, tt)])

        if kind > 0:
            P.dma("sync", wout_sb, WS[l - 1]["wout"].rearrange("(kc p) d -> p kc d", p=128), [], ["wout_sb"])
        xt = xs[0]
        xname = "xs0"
        xregs = ["%s_%d" % (xname, t_) for t_ in range(4)]
        src = x_in if kind == 0 else xres
        dst = y_out if kind == 2 else xres

        def load_oin(g_):
            if kind > 0 and g_ < NG:
                P.dma("sync", oin, ot[:, :, g_ * 512:(g_ + 1) * 512].rearrange("c p t -> p c t"), [], ["oin"])

        def load_x(g_, with_oin=True):
            P.dma("sync", xt, src[g_ * 512:(g_ + 1) * 512, :].rearrange("(tt p) d -> p tt d", p=128), [], xregs)
            if with_oin:
                load_oin(g_)

        def store_x(g_):
            P.dma("sync", dst[g_ * 512:(g_ + 1) * 512, :].rearrange("(tt p) d -> p tt d", p=128), xt, xregs, [])

        def mk_prefetch(g_):
            def f():
                if kind < 2:
                    store_x(g_)
                if g_ + 1 < NG:
                    load_x(g_ + 1, with_oin=False)
            return f

        load_x(0)
        for g in range(NG):
            if kind == 0:
                ffn(xt, xname, WS[0], 0, 1)
                win_step(xt, xname, 0, g, mk_prefetch(g))
            elif kind == 1:
                wout_step(xt, xname, l - 1, g)
                load_oin(g + 1)
                ffn(xt, xname, WS[l - 1], l - 1, 2)
                ffn(xt, xname, WS[l], l, 1)
                win_step(xt, xname, l, g, mk_prefetch(g))
            else:
                wout_step(xt, xname, l - 1, g)
                load_oin(g + 1)
                ffn(xt, xname, WS[l - 1], l - 1, 2)
                for tt in range(4):
                    xr, xnr, sr = "%s_%d" % (xname, tt), "xn_%d" % tt, "ss4_%d" % tt
                    ACT(xn[:, tt, :], xt[:, tt, :], AF.Square, [xr], [xnr, sr], accum_out=ss4[:, tt:tt + 1])
                    ACT(ss4[:, tt:tt + 1], ss4[:, tt:tt + 1], AF.Sqrt, [sr, "epsc"], [sr], scale=1.0 / D_MODEL, bias=epsc[:, 0:1])
                    RECIP(ss4[:, tt:tt + 1], ss4[:, tt:tt + 1], [sr], [sr])
                    STT("dve", xt[:, tt, :], xt[:, tt, :], ss4[:, tt:tt + 1], fnorm, ALU.mult, ALU.mult,
                        [xr, sr, "fnorm"], [xr])
                store_x(g)
                mk_prefetch(g)()
        P.barrier()

    def attn_phase(l):
        A.reset()
        stg_a = [A.alloc([WIN_COLS]) for _ in range(3)]
        items = [(l, "wout"), (l, "wg2"), (l, "wu2"), (l, "wd2")]
        if l + 1 < L:
            items += [(l + 1, "wg1"), (l + 1, "wu1"), (l + 1, "wd1"), (l + 1, "win")]
        convert_weights(stg_a, items)
        sm = smalls[l]
        smn = "small%d" % l
        LA = int(os.environ.get("KLA", "2"))
        pipe = []

        def pipe_push(fn):
            pipe.append(fn)
            while len(pipe) > LA:
                pipe.pop(0)()

        def pipe_flush():
            while pipe:
                pipe.pop(0)()

        banks = {}
        cnt = {}
        tiles = {}

        def set_banks(ns, no, nm):
            banks["s"] = [(pb[i], "pb%d" % i) for i in range(ns)]
            banks["o"] = [(pb[ns + i], "pb%d" % (ns + i)) for i in range(no)]
            banks["m"] = [(pb[ns + no + i], "pb%d" % (ns + no + i)) for i in range(nm)]
            assert ns + no + nm <= 8
            for k in ("s", "o", "m", "rz", "os"):
                cnt[k] = 0
            tiles["pt"] = [A.alloc([512]) for _ in range(ns)]
            tiles["rz"] = [A.alloc([512], F32) for _ in range(3)]
            tiles["os"] = [A.alloc([512]) for _ in range(3)]

        def nxt(key, n):
            i = cnt[key] % n
            cnt[key] += 1
            return i

        def next_s():
            i = nxt("s", len(banks["s"]))
            return banks["s"][i][0], banks["s"][i][1], tiles["pt"][i], "pt%d" % i

        def next_o():
            return banks["o"][nxt("o", len(banks["o"]))]

        def next_m():
            return banks["m"][nxt("m", len(banks["m"]))]

        def next_rz():
            i = nxt("rz", 3)
            return tiles["rz"][i], "rzt%d" % i

        def next_os():
            i = nxt("os", 3)
            return tiles["os"][i], "ostage%d" % i

        def mk_pv(O, On, c0, c1, Vp, vname, kc, Pt, Ptn, first, last, fin):
            def pv():
                MM(O[:, c0:c1], Vp[:, kc, :], Pt[:, c0:c1], first, last, [vname, Ptn], [On])
                if last and fin is not None:
                    fin()
            return pv

        def dense_causal(kT, kname, qT, qname, dk, kb, Vp, vname, scale, qg, O, On, extra=None, win=False, fin=None):
            chunks = []
            if win:
                for j in range(4):
                    chunks.append((4 * qg + j, "d", j))
                if qg > 0:
                    for j in range(4):
                        chunks.append((4 * qg - 4 + j, "w", j))
            else:
                for kc in range(4 * qg):
                    chunks.append((kc, "f", 0))
                for j in range(4):
                    chunks.append((4 * qg + j, "d", j))
            assert chunks[0][1] == "f" or (chunks[0][1] == "d" and chunks[0][2] == 0)
            nchunks = len(chunks)
            q0 = qg * 512
            for idx, (kc, typ, j) in enumerate(chunks):
                if typ == "f":
                    c0, c1 = 0, 512
                elif typ == "d":
                    c0, c1 = 128 * j, 512
                else:
                    c0, c1 = 0, 128 * (j + 1)
                S, Sn, Pt, Ptn = next_s()
                has_extra = extra is not None
                has_mask = typ in ("d", "w")
                MM(S[:, c0:c1], kT[kb:kb + dk, kc * 128:(kc + 1) * 128], qT[kb:kb + dk, q0 + c0:q0 + c1], True,
                   not (has_extra or has_mask), [kname, qname], [Sn])
                if has_extra:
                    extra(kc, S, Sn, c0, c1, not has_mask)
                if has_mask:
                    mt, mtn = (ntri, "ntri") if typ == "d" else (ntriw, "ntriw")
                    MM(S[:, 128 * j:128 * j + 128], ident, mt, False, True, ["ident", mtn], [Sn])
                ACT(Pt[:, c0:c1], S[:, c0:c1], AF.Exp, [Sn], [Ptn], scale=scale)
                pipe_push(mk_pv(O, On, c0, c1, Vp, vname, kc, Pt, Ptn, idx == 0, idx == nchunks - 1, fin))

        def make_vp(Vp, vpname, Vall, vallname, c0):
            CP("dve", Vp[:, :, 0:64], Vall[:, :, c0:c0 + 64], [vallname], [vpname])

        set_banks(3, 3, 2)
        Vall = A.alloc([NT, 384])
        P.dma("sync", Vall, tmv[:, 512:896].rearrange("(tt p) c -> p tt c", p=128), [], ["Vall"])
        qTs = [A.alloc([T]) for _ in range(2)]
        kTs = [A.alloc([T]) for _ in range(2)]
        Vps = [A.alloc([NT, 128]) for _ in range(2)]
        for i in range(2):
            MS("dve", Vps[i], 1.0, ["Vp%d" % i])
        sc_a = 96.0 ** -0.5

        def mla_load(h):
            s = h % 2
            qT, kT, Vp = qTs[s], kTs[s], Vps[s]
            qn, kn, vn = "qT%d" % s, "kT%d" % s, "Vp%d" % s
            P.dma("sync", qT[0:64], fm[FM_QN + h // 2, 64 * (h % 2):64 * (h % 2) + 64, :], [], [qn])
            if h < 4:
                P.dma("sync", qT[64:96], fm[FM_QR, 32 * h:32 * h + 32, :], [], [qn])
            else:
                P.dma("sync", qT[64:96], fm[FM_QR + 1, 32 * (h - 4):32 * (h - 4) + 32, :], [], [qn])
            P.dma("sync", kT[0:64], fm[FM_KN + h // 2, 64 * (h % 2):64 * (h % 2) + 64, :], [], [kn])
            P.dma("sync", kT[64:96], fm[FM_RA, 96:128, :], [], [kn])
            make_vp(Vp, vn, Vall, "Vall", 64 * h)

        def mla_fin(O, On, h, qg):
            def fin():
                rz, rzn = next_rz()
                RECIP(rz[64:128], O[64:128], [On], [rzn])
                os_, osn = next_os()
                TT("dve", os_[0:64], O[0:64], rz[64:128], ALU.mult, [On, rzn], [osn])
                P.dma("sync", ot[h // 2, 64 * (h % 2):64 * (h % 2) + 64, qg * 512:(qg + 1) * 512], os_[0:64], [osn], [])
            return fin

        mla_load(0)
        for h in range(6):
            if h + 1 < 6:
                pipe_flush()
                mla_load(h + 1)
            s = h % 2
            for qg in range(NG):
                O, On = next_o()
                dense_causal(kTs[s], "kT%d" % s, qTs[s], "qT%d" % s, 96, 0, Vps[s], "Vp%d" % s, sc_a, qg, O, On,
                             fin=mla_fin(O, On, h, qg))
        pipe_flush()
        P.barrier(skip_dma_of=("pool",))
        if os.environ.get("KSTOP") == "mla":
            raise StopBuild()

        A.reset()
        _stg_keep = [A.alloc([WIN_COLS]) for _ in range(3)]
        set_banks(3, 4, 1)
        Vall = A.alloc([NT, 256])
        P.dma("sync", Vall, tmv[:, 256:512].rearrange("(tt p) c -> p tt c", p=128), [], ["Vall"])
        qTs = [A.alloc([T]) for _ in range(2)]
        kTs = [[A.alloc([T]) for _ in range(2)] for _ in range(2)]
        Vps = [A.alloc([NT, 128]) for _ in range(2)]
        o12 = [[A.alloc([512], F32) for _ in range(2)] for _ in range(2)]
        dds = [A.alloc([512], F32) for _ in range(2)]
        sqds = [A.alloc([512]) for _ in range(2)]
        for i in range(2):
            MS("dve", Vps[i], 1.0, ["Vp%d" % i])
            MS("dve", qTs[i], 0.0, ["qT%d" % i])
            for m_ in range(2):
                MS("dve", kTs[i][m_], 0.0, ["kT%d_%d" % (i, m_)])
        sc_c = 32.0 ** -0.5

        def diff_load(hh):
            s = hh % 2
            qT, Vp = qTs[s], Vps[s]
            qn, vn = "qT%d" % s, "Vp%d" % s
            for mm_ in range(2):
                idx = 2 * hh + mm_
                kT, kn = kTs[s][mm_], "kT%d_%d" % (s, mm_)
                P.dma("sync", qT[32 * mm_:32 * mm_ + 8], fm[FM_RC, 8 * idx:8 * idx + 8, :], [], [qn])
                P.dma("sync", qT[32 * mm_ + 8:32 * mm_ + 32], fm[FM_DQ + idx // 4, 24 * (idx % 4):24 * (idx % 4) + 24, :], [], [qn])
                P.dma("sync", kT[32 * mm_:32 * mm_ + 8], fm[FM_RC, 64 + 8 * idx:64 + 8 * idx + 8, :], [], [kn])
                P.dma("sync", kT[32 * mm_ + 8:32 * mm_ + 32], fm[FM_DK + idx // 4, 24 * (idx % 4):24 * (idx % 4) + 24, :], [], [kn])
            make_vp(Vp, vn, Vall, "Vall", 64 * hh)

        def diff_fin1(O, On, par, mm_):
            def fin():
                rz, rzn = next_rz()
                RECIP(rz[64:128], O[64:128], [On], [rzn])
                TT("dve", o12[par][mm_][0:64], O[0:64], rz[64:128], ALU.mult, [On, rzn], ["o12_%d%d" % (par, mm_)])
            return fin

        def diff_fin2(O, On, par, hh, qg):
            f1 = diff_fin1(O, On, par, 1)

            def fin():
                f1()
                dd, ddn = dds[par], "dd%d" % par
                sqd, sqn = sqds[par], "sqd%d" % par
                STT("dve", dd[0:64], o12[par][1][0:64], neglam[l][0:64, 0:1], o12[par][0][0:64], ALU.mult, ALU.add,
                    ["o12_%d0" % par, "o12_%d1" % par, "neglam%d" % l], [ddn])
                ACT(sqd[0:64], dd[0:64], AF.Square, [ddn], [sqn])

                def finb():
                    M_, Mn = next_m()
                    MM(M_[0:64], onesb[0:64, 0:64], sqd[0:64], True, True, [sqn, "onesb"], [Mn])
                    rz, rzn = next_rz()
                    ACT(rz[0:64], M_[0:64], AF.Sqrt, [Mn, "epsc"], [rzn], scale=1.0 / 64.0, bias=epsc[0:64, 0:1])
                    RECIP(rz[0:64], rz[0:64], [rzn], [rzn])
                    os_, osn = next_os()
                    STT("dve", os_[0:64], dd[0:64], gsub[l][0:64, 0:1], rz[0:64], ALU.mult, ALU.mult,
                        [ddn, "gsub%d" % l, rzn], [osn])
                    P.dma("sync", ot[6 + hh // 2, 64 * (hh % 2):64 * (hh % 2) + 64, qg * 512:(qg + 1) * 512], os_[0:64], [osn], [])
                pipe_push(finb)
            return fin

        diff_load(0)
        npar = 0
        for hh in range(4):
            if hh + 1 < 4:
                pipe_flush()
                diff_load(hh + 1)
            s = hh % 2
            for qg in range(NG):
                par = npar % 2
                npar += 1
                for mm_ in range(2):
                    O, On = next_o()
                    fin = diff_fin1(O, On, par, 0) if mm_ == 0 else diff_fin2(O, On, par, hh, qg)
                    dense_causal(kTs[s][mm_], "kT%d_%d" % (s, mm_), qTs[s], "qT%d" % s, 128, 0, Vps[s], "Vp%d" % s, sc_c, qg, O, On, fin=fin)
        pipe_flush()
        P.barrier()
        if os.environ.get("KSTOP") == "diff":
            raise StopBuild()

        A.reset()
        set_banks(3, 2, 2)
        CO, COn = pb[7], "pb7"
        cpt = [A.alloc([512]) for _ in range(2)]
        Emat = A.alloc([T])
        cmask = A.alloc([NCC, T])
        phik1_l = A.alloc([32, 64])
        phiv1_l = A.alloc([32, 64])
        MS("pool", Emat, 1.0, ["Emat"])
        ASEL(Emat, Emat, [[1, T]], ALU.is_ge, 0.0, 0, -64, ["Emat"], ["Emat"])
        ASEL(Emat, Emat, [[-1, T]], ALU.is_ge, 0.0, 63, 64, ["Emat"], ["Emat"])
        MS("pool", cmask, 1.0, ["cmask"])
        for cc in range(NCC):
            ASEL(cmask[:, cc, :], cmask[:, cc, :], [[1, T]], ALU.is_ge, 0.0, -31 - 2048 * cc, -16, ["cmask"], ["cmask"])
        P.dma("pool", phik1_l[0:64], W[l]["phik1"].rearrange("(l d) j -> d l j", d=64), (), ["phik1"])
        P.dma("pool", phiv1_l[0:64], W[l]["phiv1"].rearrange("(l d) j -> d l j", d=64), (), ["phiv1"])
        gT = A.alloc([T])
        MS("pool", gT, 0.0, ["gT"])
        P.dma("sync", gT[0:18], fm[FM_GT, 0:18, :], [], ["gT"])
        Gsel = A.alloc([18, 64])
        MS("pool", Gsel, 1.0, ["Gsel"])
        for j in range(18):
            ASEL(Gsel[:, j, :], Gsel[:, j, :], [[0, 64]], ALU.is_equal, 0.0, -j, 1, ["Gsel"], ["Gsel"])
        VallS = A.alloc([NT, 256])
        P.dma("sync", VallS, tmv[:, 0:256].rearrange("(tt p) c -> p tt c", p=128), [], ["VallS"])
        qTn = [A.alloc([T]) for _ in range(3)]
        ksT = A.alloc([T])
        kwT = A.alloc([T])
        kcT = A.alloc([T])
        vcT = A.alloc([T])
        VpS = A.alloc([NT, 128])
        VpW = A.alloc([NT, 128])
        MS("pool", VpS, 1.0, ["VpS"])
        MS("pool", VpW, 1.0, ["VpW"])
        for hg_ in range(3):
            MS("pool", qTn[hg_], 0.0, ["qTn%d" % hg_])
        MS("pool", ksT, 0.0, ["ksT"])
        MS("pool", kwT, 0.0, ["kwT"])
        kcmpT = A.alloc([NCC * 128])
        VpC = A.alloc([NCC, 128])
        hk = A.alloc([NCC * 128])
        hv = A.alloc([NCC * 128])
        biask = A.alloc([1], F32)
        biasv = A.alloc([1], F32)
        negbTs = [A.alloc([512]) for _ in range(2)]
        for i_ in range(2):
            MS("pool", negbTs[i_], 0.0, ["negbT%d" % i_])
        score = A.alloc([4, 64], F32)
        scw = A.alloc([4, 64], F32)
        impsb = A.alloc([260], F32)
        FBpat = A.alloc([4, 10], F32)
        MS("dve", FBpat, 0.0, ["FBpat"])
        for tt_ in range(4):
            MS("dve", FBpat[0:64, tt_, 2 * tt_ + 1:2 * tt_ + 2], 20.0, ["FBpat"])
            MS("dve", FBpat[0:64, tt_, 2 * tt_:2 * tt_ + 1], 40.0, ["FBpat"])
            MS("dve", FBpat[64:128, tt_, 2 * tt_ + 2:2 * tt_ + 3], 20.0, ["FBpat"])
            MS("dve", FBpat[64:128, tt_, 2 * tt_ + 1:2 * tt_ + 2], 40.0, ["FBpat"])
        mx8 = A.alloc([4, 16], F32)
        rzc = A.alloc([4, 4], F32)
        nbf = A.alloc([4, 64], F32)
        ocgs = [[A.alloc([512], F32) for _ in range(3)] for _ in range(2)]
        gbs = [A.alloc([512], F32) for _ in range(3)]
        gcnt = [0]
        sc_b = 64.0 ** -0.5

        def next_gb():
            i = gcnt[0] % 3
            gcnt[0] += 1
            return gbs[i], "gb%d" % i

        for g in range(2):
            pipe_flush()
            for hg in range(3):
                h = 3 * g + hg
                P.dma("sync", qTn[hg][0:16], fm[FM_RA, 16 * h:16 * h + 16, :], [], ["qTn%d" % hg])
                P.dma("sync", qTn[hg][16:64], fm[FM_NQ + h // 2, 48 * (h % 2):48 * (h % 2) + 48, :], [], ["qTn%d" % hg])
            for (b, tl, tn) in ((0, kcT, "kcT"), (1, ksT, "ksT"), (2, kwT, "kwT")):
                P.dma("sync", tl[0:16], fm[FM_RB, 16 * (2 * b + g):16 * (2 * b + g) + 16, :], [], [tn])
                P.dma("sync", tl[16:64], fm[FM_NK + b, 48 * g:48 * g + 48, :], [], [tn])
            P.dma("sync", vcT[0:64], fm[FM_VC, 64 * g:64 * g + 64, :], [], ["vcT"])
            make_vp(VpS, "VpS", VallS, "VallS", 64 * g)
            make_vp(VpW, "VpW", VallS, "VallS", 128 + 64 * g)
            MS("pool", kcmpT, 0.0, ["kcmpT"])
            MS("pool", VpC, 0.0, ["VpC"])
            MS("pool", VpC[:, :, 64:128], 1.0, ["VpC"])
            MS("pool", hk, 0.0, ["hk"])
            MS("pool", hv, 0.0, ["hv"])
            for (srcT, srcn, w1, w1n, w2, w2n, peT, pen, hh_, hn, bias_, bn, isk) in (
                    (kcT, "kcT", phik1_l, "phik1", phik2_sb[l], "phik2%d" % l, pekT[l], "pekT%d" % l, hk, "hk", biask, "biask", True),
                    (vcT, "vcT", phiv1_l, "phiv1", phiv2_sb[l], "phiv2%d" % l, pevT[l], "pevT%d" % l, hv, "hv", biasv, "biasv", False)):
                Mb, Mbn = next_m()
                for li in range(32):
                    MM(Mb[0:64, 0:1], w1[0:64, li, :], peT[0:64, li:li + 1], li == 0, li == 31, [w1n, pen], [Mbn])
                CP("dve", bias_[0:64], Mb[0:64, 0:1], [Mbn], [bn])
                Mp, Mpn = next_m()
                for li in range(32):
                    MM(Mp[0:64, 0:NCMP], w1[0:64, li, :], srcT[0:64, li:li + 16 * (NCMP - 1) + 1:16], li == 0, li == 31, [w1n, srcn], [Mpn])
                ACT(hh_[0:64, 0:NCMP], Mp[0:64, 0:NCMP], AF.Silu, [Mpn, bn], [hn], bias=bias_[0:64, 0:1])
                if isk:
                    M2, M2n = next_m()
                    MM(M2[0:64, 0:NCC * 128], w2[0:64, :], hh_[0:64, :], True, True, [w2n, hn], [M2n])
                    CP("dve", kcmpT[0:64, 0:NCMP], M2[0:64, 0:NCMP], [M2n], ["kcmpT"])
                else:
                    for cc in range(NCC):
                        rows = min(128, NCMP - 128 * cc)
                        M2, M2n = next_m()
                        MM(M2[0:rows, 0:64], hh_[0:64, cc * 128:cc * 128 + rows], w2[0:64, :], True, True, [w2n, hn], [M2n])
                        CP("dve", VpC[0:rows, cc, 0:64], M2[0:rows, 0:64], [M2n], ["VpC"])

            def stage_a1(qg):
                pipe_flush()
                q0 = qg * 512
                par = qg % 2
                for hg in range(3):
                    h = 3 * g + hg
                    qT, qn = qTn[hg], "qTn%d" % hg
                    ccs = [cc for cc in range(NCC) if 16 * (128 * cc) + 31 <= q0 + 511]
                    O, On = CO, COn
                    ets = []
                    for ci, cc in enumerate(ccs):
                        S, Sn = next_m()
                        Pt, Ptn = cpt[ci], "cpt%d" % ci
                        MM(S, kcmpT[:, cc * 128:(cc + 1) * 128], qT[:, q0:q0 + 512], True, True, ["kcmpT", qn], [Sn])
                        ACT(Pt, S, AF.Exp, [Sn], [Ptn], scale=sc_b)
                        TT("dve", Pt, Pt, cmask[:, cc, q0:q0 + 512], ALU.mult, [Ptn, "cmask"], [Ptn])
                        ets.append((cc, Pt, Ptn))
                    for ci, (cc, Pt, Ptn) in enumerate(ets):
                        MM(O, VpC[:, cc, :], Pt, ci == 0, ci == len(ets) - 1, ["VpC", Ptn], [On])
                    M_, Mn = next_m()
                    Mv = M_[:, 0:260].rearrange("p (t c) -> p t c", c=65)
                    for tt in range(4):
                        for ci, (cc, Pt, Ptn) in enumerate(ets):
                            MM(M_[:, tt * 65:tt * 65 + 65], Pt[:, tt * 128:(tt + 1) * 128], OV[:, cc, :], ci == 0, ci == len(ets) - 1, [Ptn, "OV"], [Mn])
                    CP("dve", impsb, M_[:, 0:260], [Mn], ["impsb"])
                    Ms = impsb.rearrange("p (t c) -> p t c", c=65)
                    TS("dve", rzc[:, :, hg], Ms[:, :, 64], tinyc[:, 0:1], None, ALU.add, None, ["impsb", "tinyc"], ["rzc"])
                    RECIP(rzc[:, :, hg], rzc[:, :, hg], ["rzc"], ["rzc"])
                    rzb = rzc[:, :, hg:hg + 1].to_broadcast([128, 4, 64])
                    if hg == 0:
                        TT("dve", score, Ms[:, :, 0:64], rzb, ALU.mult, ["impsb", "rzc"], ["score"])
                    else:
                        TT("dve", scw, Ms[:, :, 0:64], rzb, ALU.mult, ["impsb", "rzc"], ["scw"])
                        TT("dve", score, score, scw, ALU.add, ["score", "scw"], ["score"])
                    rz, rzn = next_rz()
                    TS("dve", rz[64:128], O[64:128], tinyc[64:128, 0:1], None, ALU.add, None, [On, "tinyc"], [rzn])
                    RECIP(rz[64:128], rz[64:128], [rzn], [rzn])
                    Gp, Gpn = next_m()
                    MM(Gp[0:64], Gsel[:, 3 * h + 0, :], gT[:, q0:q0 + 512], True, True, ["Gsel", "gT"], [Gpn])
                    gb_, gbn = next_gb()
                    TT("dve", gb_[0:64], Gp[0:64], rz[64:128], ALU.mult, [Gpn, rzn], [gbn])
                    TT("dve", ocgs[par][hg][0:64], O[0:64], gb_[0:64], ALU.mult, [On, gbn], ["ocg%d%d" % (par, hg)])
                TS("dve", score[:, :, 0:1], score[:, :, 0:1], 10.0, None, ALU.add, None, ["score"], ["score"])
                if qg == 0:
                    TT("dve", score[:, :, 0:8], score[:, :, 0:8], FBpat[:, :, 1:9], ALU.add, ["score", "FBpat"], ["score"])
                else:
                    TT("dve", score[:, :, 8 * qg - 1:8 * qg + 8], score[:, :, 8 * qg - 1:8 * qg + 8], FBpat[:, :, 0:9], ALU.add,
                       ["score", "FBpat"], ["score"])
                for tt in range(4):
                    P.op("dve", lambda e, o=mx8[:, tt, 0:8], i=score[:, tt, 0:NSLC]: e.max(out=o, in_=i), ["score"], ["mx8"])
                    P.op("dve", lambda e, o=scw[:, tt, 0:NSLC], r=mx8[:, tt, 0:8], i=score[:, tt, 0:NSLC]:
                         e.match_replace(out=o, in_to_replace=r, in_values=i, imm_value=-1e30), ["score", "mx8"], ["scw"])
                    P.op("dve", lambda e, o=mx8[:, tt, 8:16], i=scw[:, tt, 0:NSLC]: e.max(out=o, in_=i), ["scw"], ["mx8"])
                    TS("dve", nbf[:, tt, 0:NSLC], score[:, tt, 0:NSLC], mx8[:, tt, 15:16], -30000.0, ALU.is_lt, ALU.mult, ["score", "mx8"], ["nbf"])

            def stage_a2(qg):
                par = qg % 2
                for tt in range(4):
                    M_, Mn = next_m()
                    P.op("pe", lambda e, o=M_[0:NSLC, 0:128], i=nbf[:, tt, 0:NSLC]: e.transpose(out=o, in_=i, identity=identf),
                         ["nbf", "identf"], [Mn])
                    CP("dve", negbTs[par][0:NSLC, tt * 128:(tt + 1) * 128], M_[0:NSLC, 0:128], [Mn], ["negbT%d" % par])

            def mk_extra(par):
                def slc_extra(kc, S, Sn, c0, c1, last):
                    MM(S[:, c0:c1], Emat[:, kc * 128:(kc + 1) * 128], negbTs[par][:, c0:c1], False, last,
                       ["Emat", "negbT%d" % par], [Sn])
                return slc_extra

            def nsa_fin(O, On, h, hg, br, qg):
                par = qg % 2
                q0 = qg * 512

                def fin():
                    rz, rzn = next_rz()
                    RECIP(rz[64:128], O[64:128], [On], [rzn])
                    Gp, Gpn = next_m()
                    MM(Gp[0:64], Gsel[:, 3 * h + 1 + br, :], gT[:, q0:q0 + 512], True, True, ["Gsel", "gT"], [Gpn])
                    gb_, gbn = next_gb()
                    TT("dve", gb_[0:64], Gp[0:64], rz[64:128], ALU.mult, [Gpn, rzn], [gbn])
                    TT("dve", gb_[0:64], O[0:64], gb_[0:64], ALU.mult, [On, gbn], [gbn])
                    ocn = "ocg%d%d" % (par, hg)
                    if br == 0:
                        TT("dve", ocgs[par][hg][0:64], ocgs[par][hg][0:64], gb_[0:64], ALU.add, [ocn, gbn], [ocn])
                    else:
                        os_, osn = next_os()
                        TT("dve", os_[0:64], ocgs[par][hg][0:64], gb_[0:64], ALU.add, [ocn, gbn], [osn])
                        P.dma("sync", ot[3 + h // 2, 64 * (h % 2):64 * (h % 2) + 64, q0:q0 + 512], os_[0:64], [osn], [])
                return fin

            def stage_b(qg, mid):
                par = qg % 2
                for hg in range(3):
                    h = 3 * g + hg
                    qT, qn = qTn[hg], "qTn%d" % hg
                    for br, (kT_, kn_, Vp_, vn_) in enumerate(((ksT, "ksT", VpS, "VpS"), (kwT, "kwT", VpW, "VpW"))):
                        O, On = next_o()
                        dense_causal(kT_, kn_, qT, qn, 128, 0, Vp_, vn_, sc_b, qg, O, On,
                                     extra=mk_extra(par) if br == 0 else None, win=(br == 1),
                                     fin=nsa_fin(O, On, h, hg, br, qg))
                    if hg == 0 and mid is not None:
                        mid()

            stage_a1(0)
            stage_a2(0)
            for qg in range(NG):
                if qg + 1 < NG:
                    stage_a1(qg + 1)
                    stage_b(qg, lambda q=qg + 1: stage_a2(q))
                else:
                    stage_b(qg, None)
        pipe_flush()
        P.barrier()

    try:
      if os.environ.get("KSTOP") == "setup":
          raise StopBuild()
      row_phase(0, 0)
      if os.environ.get("KSTOP") == "row0":
          raise StopBuild()
      for l in range(L):
        attn_phase(l)
        if os.environ.get("KSTOP") == "nsa":
            raise StopBuild()
        if dbg and l == 0:
            P.dma("sync", dbg_out["d_xres"], xres, [], [])
            P.dma("sync", dbg_out["d_fm"], fm, [], [])
            P.dma("sync", dbg_out["d_tmv"], tmv, [], [])
            P.dma("sync", dbg_out["d_ot"], ot, [], [])
            P.barrier()
        if l + 1 < L:
            row_phase(1, l + 1)
        else:
            row_phase(2, L)
    except StopBuild:
        pass
    P.emit(nc, st)
    st.close()
    return nc


def make_in_maps(inputs, T, L=DEPTH, batches=None):
    f32 = np.float32
    x = np.asarray(inputs["x"], f32)
    pos = np.asarray(inputs["positions"], np.int32)
    B = x.shape[0]
    wperm = _win_perm()
    qperm = _wuq_perm()
    kvperm = _wukv_perm()
    shared = {}
    shared["fnorm"] = np.ascontiguousarray(np.broadcast_to(np.asarray(inputs["final_norm"], f32)[None, :], (128, D_MODEL)))
    shared["ropef"] = _rope_consts()
    for l in range(L):
        g = lambda k: np.asarray(inputs[k][l], f32)
        shared["wg1_%d" % l] = np.ascontiguousarray(g("ffn1_wg"))
        shared["wu1_%d" % l] = np.ascontiguousarray(g("ffn1_wu"))
        shared["wd1_%d" % l] = np.ascontiguousarray(g("ffn1_wd"))
        shared["wg2_%d" % l] = np.ascontiguousarray(g("ffn2_wg"))
        shared["wu2_%d" % l] = np.ascontiguousarray(g("ffn2_wu"))
        shared["wd2_%d" % l] = np.ascontiguousarray(g("ffn2_wd"))
        shared["win_%d" % l] = _take_cols(g("w_in"), wperm)
        shared["wuq_%d" % l] = _take_cols(g("mla_w_uq"), qperm)
        shared["wukv_%d" % l] = _take_cols(g("mla_w_ukv"), kvperm)
        shared["wout_%d" % l] = np.ascontiguousarray(g("w_out"))
        shared["phik1_%d" % l] = np.ascontiguousarray(g("nsa_phi_k1"))
        shared["phiv1_%d" % l] = np.ascontiguousarray(g("nsa_phi_v1"))
        shared["phik2_%d" % l] = np.ascontiguousarray(g("nsa_phi_k2"))
        shared["phiv2_%d" % l] = np.ascontiguousarray(g("nsa_phi_v2"))
        sm = np.zeros((128, NSMALL), f32)
        sm[:, 0:8] = g("ffn1_norm").reshape(8, 128).T
        sm[:, 8:16] = g("mix_norm").reshape(8, 128).T
        sm[:, 16:24] = g("ffn2_norm").reshape(8, 128).T
        qn = g("mla_q_norm")
        sm[:, 24] = qn[0:128]
        sm[0:64, 25] = qn[128:192]
        sm[:, 26] = g("mla_kv_norm")
        sn = g("diff_sub_norm")
        sm[0:64, 27] = sn
        sm[64:128, 27] = sn
        sm[:, 28:60] = g("diff_lq1")[None, :]
        sm[:, 60:92] = g("diff_lk1")[None, :]
        sm[:, 92:124] = g("diff_lq2")[None, :]
        sm[:, 124:156] = g("diff_lk2")[None, :]
        sm[0:64, 156:188] = g("nsa_pe_k").T
        sm[0:64, 188:220] = g("nsa_pe_v").T
        shared["small_%d" % l] = sm
    maps = []
    for b in (batches if batches is not None else range(B)):
        m = dict(shared)
        m["x"] = np.ascontiguousarray(x[b])
        m["pos"] = np.ascontiguousarray(pos[b][None, :])
        maps.append(m)
    return maps


_NC_CACHE = {}


def kernel(**inputs):
    x = np.asarray(inputs["x"])
    B, T, _ = x.shape
    if T not in _NC_CACHE:
        _NC_CACHE[T] = build(T, DEPTH)
    nc = _NC_CACHE[T]
    maps = make_in_maps(inputs, T)
    res = run_bass_kernel_spmd(nc, maps, core_ids=list(range(B)))
    out = np.stack([np.asarray(r["y"]) for r in res.results], axis=0)
    return out.astype(np.float32)
```
